# Optimizing a Trainium2 kernel written in Bass

```python
import jax
import jax.numpy as jnp
from jax import lax
import numpy as np

D_MODEL = 1024
BATCH = 8
SEQ = 4096
DEPTH = 1

MIX_WIDTH = D_MODEL
HG_WIDTH = MIX_WIDTH // 2
HG_HEAD_DIM = 128
HG_HEADS = HG_WIDTH // HG_HEAD_DIM
HG_EXPAND = 128
HG_FDIM = HG_HEADS * HG_EXPAND
HG_CHUNK = 64
ATT_WIDTH = MIX_WIDTH - HG_WIDTH
ATT_HEAD_DIM = 64
ATT_HEADS = ATT_WIDTH // ATT_HEAD_DIM
DILATED_PAIRS = ((128, 1), (512, 4), (2048, 16))
ATT_BLOCK = 128
D_FF = ((8 * D_MODEL + 3 * 256 - 1) // (3 * 256)) * 256
RMS_EPS = 1e-6
IN_SIZES = (HG_FDIM, HG_FDIM, HG_WIDTH, HG_WIDTH, ATT_WIDTH, ATT_WIDTH, ATT_WIDTH)
IN_WIDTH = HG_FDIM * 2 + HG_WIDTH * 2 + ATT_WIDTH * 3

kernel_name = 'hybrid_hgrn2_dilated_attn_adaln_block'


def rms_norm(x, g):
    xf = x.astype(jnp.float32)
    y = xf * lax.rsqrt(jnp.mean(xf * xf, axis=-1, keepdims=True) + RMS_EPS)
    return (y * g.astype(jnp.float32)).astype(x.dtype)


def modulate(h, shift, scale):
    return h * (1 + scale[:, None, :]) + shift[:, None, :]


def hgrn2_mixer(q, f_raw, i, g, lb, onorm_g):
    B, S = q.shape[0], q.shape[1]
    nc = S // HG_CHUNK
    lb = lb.reshape(HG_HEADS, HG_EXPAND)
    f = lb + (1.0 - lb) * jax.nn.sigmoid(f_raw.astype(jnp.float32))
    k = 1.0 - f
    log_f = jnp.log(f)
    qf = jax.nn.silu(q.astype(jnp.float32))
    vf = i.astype(jnp.float32)

    def chunks(t):
        return t.reshape(B, nc, HG_CHUNK, HG_HEADS, t.shape[-1]).transpose(1, 0, 3, 2, 4)

    qc, kc, vc = chunks(qf), chunks(k), chunks(vf)
    bc = jnp.cumsum(chunks(log_f), axis=3)
    causal = jnp.tril(jnp.ones((HG_CHUNK, HG_CHUNK), dtype=bool))

    def step(state, inp):
        q_c, k_c, v_c, b_c = inp
        o_inter = jnp.einsum('bhtk,bhkv->bhtv', q_c * jnp.exp(b_c), state)
        diff = b_c[:, :, :, None, :] - b_c[:, :, None, :, :]
        decay = jnp.where(causal[:, :, None], jnp.exp(jnp.minimum(diff, 0.0)), 0.0)
        scores = jnp.einsum('bhtk,bhsk,bhtsk->bhts', q_c, k_c, decay)
        o_intra = jnp.einsum('bhts,bhsv->bhtv', scores, v_c)
        b_last = b_c[:, :, -1, :]
        k_dec = k_c * jnp.exp(b_last[:, :, None, :] - b_c)
        state = jnp.exp(b_last)[..., None] * state + jnp.einsum('bhsk,bhsv->bhkv', k_dec, v_c)
        return state, o_inter + o_intra

    state0 = jnp.zeros((B, HG_HEADS, HG_EXPAND, HG_HEAD_DIM), jnp.float32)
    _, o = lax.scan(step, state0, (qc, kc, vc, bc))
    o = o.transpose(1, 0, 3, 2, 4).reshape(B, S, HG_HEADS, HG_HEAD_DIM)
    o = rms_norm(o, onorm_g) * jax.nn.silu(g.astype(jnp.float32))
    return o.reshape(B, S, HG_WIDTH).astype(q.dtype)


def dilated_branch(q, k, v, window, dil):
    B, H, S, E = q.shape
    span = window // dil
    seg = dil * ATT_BLOCK
    s_pad = -(-S // seg) * seg
    m = s_pad // dil
    nb = m // ATT_BLOCK

    def to_blocks(t):
        t = jnp.pad(t, ((0, 0), (0, 0), (0, s_pad - S), (0, 0)))
        t = t.reshape(B, H, m, dil, E).transpose(0, 1, 3, 2, 4)
        return t.reshape(B, H, dil, nb, ATT_BLOCK, E)

    def with_prev(t):
        prev = jnp.pad(t[:, :, :, :-1], ((0, 0), (0, 0), (0, 0), (1, 0), (0, 0), (0, 0)))
        return jnp.concatenate([prev, t], axis=4)

    qb = to_blocks(q)
    kb = with_prev(to_blocks(k))
    vb = with_prev(to_blocks(v))
    s = jnp.einsum('bhrnqe,bhrnke->bhrnqk', qb, kb).astype(jnp.float32)
    qi = jnp.arange(ATT_BLOCK)[:, None]
    kj = jnp.arange(2 * ATT_BLOCK)[None, :]
    dist = ATT_BLOCK + qi - kj
    band = (dist >= 0) & (dist <= span)
    real = (jnp.arange(nb) > 0)[:, None, None] | (kj >= ATT_BLOCK)[None]
    mask = band[None] & real
    s = jnp.where(mask, s, -jnp.inf)
    s_max = jnp.max(s, axis=-1, keepdims=True)
    p = jnp.exp(s - s_max)
    l = jnp.sum(p, axis=-1, keepdims=True)
    o = jnp.einsum('bhrnqk,bhrnke->bhrnqe', p, vb.astype(jnp.float32)) / l
    lse = (s_max + jnp.log(l))[..., 0]
    o = o.reshape(B, H, dil, m, E).transpose(0, 1, 3, 2, 4).reshape(B, H, s_pad, E)[:, :, :S]
    lse = lse.reshape(B, H, dil, m).transpose(0, 1, 3, 2).reshape(B, H, s_pad)[:, :, :S]
    return o, lse


def dilated_attention(q, k, v):
    B, S = q.shape[0], q.shape[1]
    qh = q.transpose(0, 2, 1, 3) * (ATT_HEAD_DIM ** -0.5)
    kh = k.transpose(0, 2, 1, 3)
    vh = v.transpose(0, 2, 1, 3)
    outs, lses = zip(*[dilated_branch(qh, kh, vh, w, d) for (w, d) in DILATED_PAIRS])
    weights = jax.nn.softmax(jnp.stack(lses), axis=0)
    o = jnp.sum(weights[..., None] * jnp.stack(outs), axis=0)
    return o.transpose(0, 2, 1, 3).reshape(B, S, ATT_WIDTH).astype(q.dtype)


def setup_inputs(seed: int = 0) -> dict:
    key = jax.random.key(seed)
    ks = jax.random.split(key, 14)

    def nrm(k, shape, scale):
        return jax.random.normal(k, shape, jnp.float32) * scale

    return {
        'x': nrm(ks[0], (BATCH, SEQ, D_MODEL), 1.0),
        'c': nrm(ks[1], (BATCH, D_MODEL), 1.0),
        'w_ada': nrm(ks[2], (DEPTH, D_MODEL, 6 * D_MODEL), D_MODEL ** -0.5),
        'b_ada': nrm(ks[3], (DEPTH, 6 * D_MODEL), 0.01),
        'norm1_g': 1.0 + nrm(ks[4], (DEPTH, D_MODEL), 0.01),
        'w_in': nrm(ks[5], (DEPTH, D_MODEL, IN_WIDTH), D_MODEL ** -0.5),
        'hg_lb_logits': nrm(ks[6], (DEPTH + 1, HG_FDIM), 0.1),
        'hg_onorm_g': 1.0 + nrm(ks[7], (DEPTH, HG_HEAD_DIM), 0.01),
        'att_onorm_g': 1.0 + nrm(ks[8], (DEPTH, ATT_WIDTH), 0.01),
        'w_out': nrm(ks[9], (DEPTH, MIX_WIDTH, D_MODEL), MIX_WIDTH ** -0.5),
        'norm2_g': 1.0 + nrm(ks[10], (DEPTH, D_MODEL), 0.01),
        'w_gate_up': nrm(ks[11], (DEPTH, D_MODEL, 2 * D_FF), D_MODEL ** -0.5),
        'w_down': nrm(ks[12], (DEPTH, D_FF, D_MODEL), D_FF ** -0.5),
        'final_g': 1.0 + nrm(ks[13], (D_MODEL,), 0.01),
    }


def reference(x, c, w_ada, b_ada, norm1_g, w_in, hg_lb_logits, hg_onorm_g, att_onorm_g,
              w_out, norm2_g, w_gate_up, w_down, final_g):
    B, S = x.shape[0], x.shape[1]
    lower_bounds = jnp.cumsum(jax.nn.softmax(hg_lb_logits.astype(jnp.float32), axis=0), axis=0)
    c_act = jax.nn.silu(c)
    split_at = np.cumsum(IN_SIZES)[:-1].tolist()
    for l in range(DEPTH):
        mod = c_act @ w_ada[l] + b_ada[l]
        shift1, scale1, gate1, shift2, scale2, gate2 = jnp.split(mod, 6, axis=-1)
        h = modulate(rms_norm(x, norm1_g[l]), shift1, scale1)
        hq, hf, hi, hgt, aq, ak, av = jnp.split(h @ w_in[l], split_at, axis=-1)
        hg_out = hgrn2_mixer(
            hq.reshape(B, S, HG_HEADS, HG_EXPAND),
            hf.reshape(B, S, HG_HEADS, HG_EXPAND),
            hi.reshape(B, S, HG_HEADS, HG_HEAD_DIM),
            hgt.reshape(B, S, HG_HEADS, HG_HEAD_DIM),
            lower_bounds[l], hg_onorm_g[l])
        att = dilated_attention(
            aq.reshape(B, S, ATT_HEADS, ATT_HEAD_DIM),
            ak.reshape(B, S, ATT_HEADS, ATT_HEAD_DIM),
            av.reshape(B, S, ATT_HEADS, ATT_HEAD_DIM))
        att_out = rms_norm(att, att_onorm_g[l])
        mix = jnp.concatenate([hg_out, att_out], axis=-1) @ w_out[l]
        x = x + gate1[:, None, :] * mix
        h = modulate(rms_norm(x, norm2_g[l]), shift2, scale2)
        a, u = jnp.split(h @ w_gate_up[l], 2, axis=-1)
        x = x + gate2[:, None, :] * ((jax.nn.silu(a) * u) @ w_down[l])
    return rms_norm(x, final_g)
```

```python
import numpy as np
from contextlib import ExitStack
import concourse.bass as bass
import concourse.mybir as mybir
from concourse.alu_op_type import AluOpType as ALU
from concourse.bass_utils import run_bass_kernel_spmd

F32 = mybir.dt.float32
BF16 = mybir.dt.bfloat16
AF = mybir.ActivationFunctionType

S = 4096
D = 1024
NT = S // 128
NST = S // 512
DFF = 2816
EPS = 1e-6


class Buf:
    __slots__ = ("t", "lw", "rd", "name", "dsem", "wrd")

    def __init__(self, t, name=""):
        self.t = t
        self.lw = None
        self.rd = {}
        self.name = name
        self.dsem = None
        self.wrd = {}

    def __getitem__(self, idx):
        return self.t[idx]


class Dep:
    __slots__ = ("sem", "val", "eng", "idx")

    def __init__(self, sem, val, eng, idx):
        self.sem = sem; self.val = val; self.eng = eng; self.idx = idx


class Em:
    def __init__(self, nc, es):
        self.nc = nc
        self.E = {"pe": nc.tensor, "act": nc.scalar, "dve": nc.vector, "pool": nc.gpsimd, "sp": nc.sync}
        self.sem = {}; self.cnt = {}; self.issued = {}; self.waited = {}
        for k in self.E:
            self.sem[k] = es.enter_context(nc.semaphore("sem_" + k))
            self.cnt[k] = 0; self.issued[k] = 0; self.waited[k] = {}
        self.es = es
        self.dma_sems = []
        self.nwaits = 0
        self.nins = 0

    def new_dma_sem(self, name):
        s = self.es.enter_context(self.nc.semaphore(name))
        d = {"sem": s, "val": 0}
        self.dma_sems.append(d)
        return d

    def _wait(self, eng, dep):
        if dep is None:
            return
        key = id(dep.sem)
        if dep.eng == eng:
            if eng == "pe":
                return
        w = self.waited[eng]
        if w.get(key, -1) >= dep.val:
            return
        if dep.eng != "dma":
            assert self.cnt[dep.eng] >= dep.val, (
                f"dep on unsignaled instr: {dep.eng} cnt={self.cnt[dep.eng]} need={dep.val} (consumer {eng})")
        self.E[eng].wait_ge(dep.sem, dep.val)
        self.nwaits += 1
        w[key] = dep.val

    def op(self, eng, fn, r=(), w=(), signal=True):
        for b in r:
            self._wait(eng, b.lw)
        for b in w:
            self._wait(eng, b.lw)
            for d in b.rd.values():
                self._wait(eng, d)
        ins = fn()
        self.nins += 1
        self.issued[eng] += 1
        tag = self.cnt[eng] + 1
        if signal:
            ins.then_inc(self.sem[eng], 1)
            self.cnt[eng] = tag
        dep = Dep(self.sem[eng], tag, eng, self.issued[eng])
        for b in r:
            b.rd[id(dep.sem)] = dep
        for b in w:
            b.lw = dep
            b.rd = {}
        return ins

    def dma(self, q, dsem, out, in_, r=(), w=(), nowaw=False, **kw):
        if isinstance(dsem, Buf):
            if dsem.dsem is None:
                dsem.dsem = self.new_dma_sem("d_" + dsem.name)
            dsem = dsem.dsem
        for b in r:
            self._wait(q, b.lw)
        for b in w:
            if nowaw and b.lw is not None and b.lw.eng == "dma" and b.lw.sem is dsem["sem"]:
                for d in b.wrd.values():
                    self._wait(q, d)
            else:
                self._wait(q, b.lw)
                b.wrd = dict(b.rd)
            for d in b.rd.values():
                self._wait(q, d)
        ins = self.E[q].dma_start(out=out, in_=in_, **kw)
        ins.then_inc(dsem["sem"], 16)
        self.nins += 1
        dsem["val"] += 16
        dep = Dep(dsem["sem"], dsem["val"], "dma", None)
        for b in r:
            b.rd[id(dep.sem)] = dep
        for b in w:
            b.lw = dep
            b.rd = {}
        return dep

    def barrier(self):
        for e in ("pe", "act", "dve", "pool"):
            for f in ("pe", "act", "dve", "pool"):
                if e == f or self.cnt[f] == 0:
                    continue
                if self.waited[e].get(id(self.sem[f]), -1) < self.cnt[f]:
                    self.E[e].wait_ge(self.sem[f], self.cnt[f])
                    self.waited[e][id(self.sem[f])] = self.cnt[f]
        for e in ("sp", "pool", "act", "dve", "pe"):
            for d in self.dma_sems:
                if d["val"] > 0 and self.waited[e].get(id(d["sem"]), -1) < d["val"]:
                    self.E[e].wait_ge(d["sem"], d["val"])
                    self.waited[e][id(d["sem"])] = d["val"]
        for f in ("pe", "act", "dve", "pool"):
            if self.cnt[f] and self.waited["sp"].get(id(self.sem[f]), -1) < self.cnt[f]:
                self.E["sp"].wait_ge(self.sem[f], self.cnt[f])
                self.waited["sp"][id(self.sem[f])] = self.cnt[f]

    def finish(self):
        for d in self.dma_sems:
            if d["val"] > 0:
                self.E["sp"].wait_ge(d["sem"], d["val"])


R_C, R_BA, R_N1, R_N2, R_LB, R_AG, R_HG = 0, 8, 56, 64, 72, 80, 84
NVR = 85


def build_nc(debug=False):
    nc = bass.Bass("TRN2", target_bir_lowering=False)

    def din(name, shape, dt=F32):
        return nc.dram_tensor(name, shape, dt, kind="ExternalInput").ap()

    x_d = din("x", [S, D])
    vecs_d = din("vecs", [NVR, 128])
    bada_d = din("b_ada", [1, 6 * D])
    wada_d = din("w_ada", [D, 6 * D])
    win_d = din("w_in", [D, 3584])
    wout_d = din("w_out", [D, D])
    wgu_d = din("w_gu", [D, 2 * DFF])
    wdn_d = din("w_down", [DFF, D])
    fg_d = din("final_g", [1, D])
    ident_d = din("ident", [128, 128])
    mhg_d = din("mask_hg", [128, 128])
    matt_d = din("mask_att", [128, 512])
    mscan_d = din("mask_scan", [128, 512])
    out_d = nc.dram_tensor("out", [S, D], F32, kind="ExternalOutput").ap()

    skind = "ExternalOutput" if debug else "Internal"
    qT_d = nc.dram_tensor("qT_scr", [8, 64, S], BF16, kind=skind).ap()
    kT_d = nc.dram_tensor("kT_scr", [8, 64, S], BF16, kind=skind).ap()
    v_d = nc.dram_tensor("v_scr", [S, 512], BF16, kind=skind).ap()
    mix_d = nc.dram_tensor("mix_scr", [NT, 128, 8, 128], BF16, kind=skind).ap()

    with ExitStack() as es:
        em = Em(nc, es)
        V, A, P, G, SP = "dve", "act", "pe", "pool", "sp"

        def sb(scope, name, shape, dt):
            return Buf(scope.enter_context(nc.sbuf_tensor(name, shape, dt)), name)

        def ps(scope, name, shape, dt):
            return Buf(scope.enter_context(nc.psum_tensor(name, shape, dt)), name)

        qT_B, kT_B, v_B, mix_B = Buf(qT_d), Buf(kT_d), Buf(v_d), Buf(mix_d)
        out_B = Buf(out_d)
        identf = sb(es, "identf", [128, 128], F32)
        identb = sb(es, "identb", [128, 128], BF16)
        mhg = sb(es, "mhg", [128, 128], F32)
        matt = sb(es, "matt", [128, 512], BF16)
        mscan = sb(es, "mscan", [128, 512], F32)
        Vc = sb(es, "Vc", [128, 96], F32)
        cst = sb(es, "cst", [128, 64], F32)
        onesf = sb(es, "onesf", [128, 128], F32)
        onesb = sb(es, "onesb", [128, 128], BF16)
        gate_bc = sb(es, "gate_bc", [128, 2 * D], F32)
        C_CACT, C_LB, C_OML, C_NOML, C_G1, C_S1, C_G2, C_S2, C_EPS = 0, 8, 12, 16, 20, 28, 36, 44, 52

        em.dma(SP, identf, identf[:], ident_d, w=[identf])
        em.dma(SP, mhg, mhg[:], mhg_d, w=[mhg])
        em.dma(SP, mscan, mscan[:], mscan_d, w=[mscan])
        em.op(V, lambda: nc.vector.tensor_copy(out=identb[:], in_=identf[:]), r=[identf], w=[identb])
        em.op(G, lambda: nc.gpsimd.memset(onesf[:], 1.0), w=[onesf])
        em.op(G, lambda: nc.gpsimd.memset(onesb[:], 1.0), w=[onesb])
        em.op(G, lambda: nc.gpsimd.memset(cst[:, C_EPS:C_EPS + 1], EPS), w=[cst])

        s1 = ExitStack()
        win_bf = sb(s1, "win_bf", [128, 8, 3584], BF16)
        s1a = ExitStack()
        stage_in = [sb(s1a, f"wstg{i}", [128, 2048], F32) for i in range(2)]
        win_tasks = []
        _wi = [0]
        for kc_ in range(8):
            for (c0_, w__) in ((0, 2048), (2048, 1536)):
                def _t(kc_=kc_, c0_=c0_, w__=w__):
                    st_ = stage_in[_wi[0] % 2]
                    e = (V, A)[_wi[0] % 2]
                    _wi[0] += 1
                    em.dma(SP, st_, st_[:, 0:w__], win_d[kc_ * 128:(kc_ + 1) * 128, c0_:c0_ + w__], w=[st_])
                    o = win_bf[:, kc_, c0_:c0_ + w__]
                    if e == V:
                        em.op(V, lambda: nc.vector.tensor_copy(out=o, in_=st_[:, 0:w__]), r=[st_], w=[win_bf])
                    else:
                        em.op(A, lambda: nc.scalar.copy(out=o, in_=st_[:, 0:w__]), r=[st_], w=[win_bf])
                win_tasks.append(_t)
        with ExitStack() as sa:
            vec_sb = sb(sa, "vec_sb", [128, 128], F32)
            mattf = sb(sa, "mattf", [128, 512], F32)
            em.dma(SP, mattf, mattf[:], matt_d, w=[mattf])
            em.op(V, lambda: nc.vector.tensor_copy(out=matt[:], in_=mattf[:]), r=[mattf], w=[matt])
            bada_bc = sb(sa, "bada_bc", [128, 6 * D], F32)
            mod_bc = sb(sa, "mod_bc", [128, 6 * D], F32)
            cbc = sb(sa, "cbc", [128, 8, 128], F32)
            stg = [sb(sa, f"wada_stg{i}", [128, 8, 1024], F32) for i in range(2)]
            psT = ps(sa, "psT", [128, 512], F32)
            psM = [ps(sa, f"psM{i}", [128, 512], F32) for i in range(2)]

            em.op(G, lambda: nc.gpsimd.memset(vec_sb[:], 0.0), w=[vec_sb])
            em.dma(SP, vec_sb, vec_sb[0:NVR, :], vecs_d, w=[vec_sb])
            em.dma(SP, bada_bc, bada_bc[:], bada_d.partition_broadcast(128), w=[bada_bc])
            wv = wada_d.rearrange("(kc p) n -> p kc n", p=128)
            em.dma(SP, stg[0], stg[0][:], wv[:, :, 0:1024], w=[stg[0]])
            em.dma(SP, stg[1], stg[1][:], wv[:, :, 1024:2048], w=[stg[1]])
            em.op(P, lambda: nc.tensor.transpose(out=psT[:, 0:128], in_=vec_sb[:], identity=identf[:]), r=[vec_sb, identf], w=[psT])
            em.op(V, lambda: nc.vector.tensor_copy(out=Vc[:, 0:96], in_=psT[:, 0:96]), r=[psT], w=[Vc])
            em.op(A, lambda: nc.scalar.activation(out=cst[:, C_CACT:C_CACT + 8], in_=Vc[:, R_C:R_C + 8], func=AF.Silu), r=[Vc], w=[cst])
            em.op(V, lambda: nc.vector.tensor_sub(out=cst[:, 56:60], in0=Vc[:, R_LB:R_LB + 4], in1=Vc[:, R_LB + 4:R_LB + 8]), r=[Vc], w=[cst])
            em.op(A, lambda: nc.scalar.activation(out=cst[:, C_LB:C_LB + 4], in_=cst[:, 56:60], func=AF.Sigmoid), r=[cst], w=[cst])
            em.op(A, lambda: nc.scalar.activation(out=cst[:, C_OML:C_OML + 4], in_=cst[:, 56:60], func=AF.Sigmoid, scale=-1.0), r=[cst], w=[cst])
            em.op(V, lambda: nc.vector.tensor_scalar(out=cst[:, C_NOML:C_NOML + 4], in0=cst[:, C_OML:C_OML + 4], scalar1=-1.0, scalar2=0.0, op0=ALU.mult, op1=ALU.add), r=[cst], w=[cst])
            for kc in range(8):
                em.op(V, lambda kc=kc: nc.vector.tensor_copy(out=cbc[:, kc, :], in_=cst[:, C_CACT + kc:C_CACT + kc + 1].to_broadcast([128, 128])), r=[cst], w=[cbc])
            for blk in range(6):
                for _ in range(3):
                    if win_tasks:
                        win_tasks.pop(0)()
                sg_ = stg[blk % 2]
                for nh in range(2):
                    pm = psM[nh]
                    for kc in range(8):
                        em.op(P, lambda kc=kc, nh=nh, sg_=sg_, pm=pm: nc.tensor.matmul(pm[:, :], lhsT=cbc[:, kc, :], rhs=sg_[:, kc, nh * 512:(nh + 1) * 512], start=(kc == 0), stop=(kc == 7)),
                              r=[cbc, sg_], w=[pm], signal=(kc == 7))
                    c0 = blk * 1024 + nh * 512
                    em.op(V, lambda c0=c0, pm=pm: nc.vector.tensor_tensor(out=mod_bc[:, c0:c0 + 512], in0=pm[:, :], in1=bada_bc[:, c0:c0 + 512], op=ALU.add), r=[pm, bada_bc], w=[mod_bc])
                if blk + 2 < 6:
                    em.dma(SP, sg_, sg_[:], wv[:, :, (blk + 2) * 1024:(blk + 3) * 1024], w=[sg_])
            em.op(V, lambda: nc.vector.tensor_tensor(out=bada_bc[:].rearrange("p (j q) -> p j q", q=128), in0=mod_bc[:].rearrange("p (j q) -> p j q", q=128),
                                                     in1=identf[:].rearrange("p (o q) -> p o q", o=1).to_broadcast([128, 48, 128]), op=ALU.mult), r=[mod_bc, identf], w=[bada_bc])
            modT = sb(sa, "modT", [128, 48], F32)
            em.op(V, lambda: nc.vector.reduce_sum(out=modT[:], in_=bada_bc[:].rearrange("p (j q) -> p j q", q=128), axis=mybir.AxisListType.X), r=[bada_bc], w=[modT])
            em.op(V, lambda: nc.vector.scalar_tensor_tensor(out=cst[:, C_G1:C_G1 + 8], in0=modT[:, 8:16], scalar=1.0, in1=Vc[:, R_N1:R_N1 + 8], op0=ALU.add, op1=ALU.mult), r=[modT, Vc], w=[cst])
            em.op(V, lambda: nc.vector.tensor_copy(out=cst[:, C_S1:C_S1 + 8], in_=modT[:, 0:8]), r=[modT], w=[cst])
            em.op(V, lambda: nc.vector.scalar_tensor_tensor(out=cst[:, C_G2:C_G2 + 8], in0=modT[:, 32:40], scalar=1.0, in1=Vc[:, R_N2:R_N2 + 8], op0=ALU.add, op1=ALU.mult), r=[modT, Vc], w=[cst])
            em.op(V, lambda: nc.vector.tensor_copy(out=cst[:, C_S2:C_S2 + 8], in_=modT[:, 24:32]), r=[modT], w=[cst])
            em.op(V, lambda: nc.vector.tensor_copy(out=gate_bc[:, 0:D], in_=mod_bc[:, 2 * D:3 * D]), r=[mod_bc], w=[gate_bc])
            em.op(V, lambda: nc.vector.tensor_copy(out=gate_bc[:, D:2 * D], in_=mod_bc[:, 5 * D:6 * D]), r=[mod_bc], w=[gate_bc])
        em.barrier()

        cast_rr = [0]

        def cast_weight(dst, src_d, KC, N, stage, fold=None):
            pieces = []
            c = 0
            while c < N:
                w_ = min(2048, N - c)
                pieces.append((c, w_))
                c += w_
            for kc in range(KC):
                for (c0, w_) in pieces:
                    i = cast_rr[0] % len(stage)
                    cast_rr[0] += 1
                    st_ = stage[i]
                    em.dma(SP, st_, st_[:, 0:w_], src_d[kc * 128:(kc + 1) * 128, c0:c0 + w_], w=[st_])
                    o = dst[:, kc, c0:c0 + w_]
                    if fold is None:
                        e = (V, A)[cast_rr[0] % 2]
                        if e == V:
                            em.op(V, lambda o=o, st_=st_, w_=w_: nc.vector.tensor_copy(out=o, in_=st_[:, 0:w_]), r=[st_], w=[dst])
                        elif e == A:
                            em.op(A, lambda o=o, st_=st_, w_=w_: nc.scalar.copy(out=o, in_=st_[:, 0:w_]), r=[st_], w=[dst])
                        else:
                            em.op(G, lambda o=o, st_=st_, w_=w_: nc.gpsimd.tensor_copy(out=o, in_=st_[:, 0:w_]), r=[st_], w=[dst])
                    else:
                        rowcol, colbc = fold
                        e = V
                        eng = nc.vector
                        if rowcol is not None:
                            rc = rowcol(kc)
                            em.op(e, lambda o=o, st_=st_, w_=w_, c0=c0, rc=rc, eng=eng: eng.scalar_tensor_tensor(out=o, in0=st_[:, 0:w_], scalar=rc, in1=colbc[:, c0:c0 + w_], op0=ALU.mult, op1=ALU.mult),
                                  r=[st_, gate_bc, Vc], w=[dst])
                        else:
                            em.op(e, lambda o=o, st_=st_, w_=w_, c0=c0, eng=eng: eng.tensor_tensor(out=o, in0=st_[:, 0:w_], in1=colbc[:, c0:c0 + w_], op=ALU.mult),
                                  r=[st_, gate_bc], w=[dst])

        def cast_tasks(dst, src_d, KC, N, stage, fold):
            tasks = []
            for kc in range(KC):
                def task(kc=kc):
                    st_ = stage[kc % len(stage)]
                    em.dma(SP, st_, st_[:, 0:N], src_d[kc * 128:(kc + 1) * 128, 0:N], w=[st_])
                    o = dst[:, kc, :]
                    rowcol, colbc = fold
                    if rowcol is not None:
                        rc = rowcol(kc)
                        em.op(V, lambda: nc.vector.scalar_tensor_tensor(out=o, in0=st_[:, 0:N], scalar=rc, in1=colbc[:, 0:N], op0=ALU.mult, op1=ALU.mult), r=[st_, gate_bc, Vc], w=[dst])
                    else:
                        em.op(V, lambda: nc.vector.tensor_tensor(out=o, in0=st_[:, 0:N], in1=colbc[:, 0:N], op=ALU.mult), r=[st_, gate_bc], w=[dst])
                tasks.append(task)
            return tasks

        while win_tasks:
            win_tasks.pop(0)()
        s1a.close()
        em.barrier()
        with s1:

            xt = [sb(s1, f"xt{i}", [128, D], F32) for i in range(2)]
            xn = [sb(s1, f"xn{i}", [128, D], BF16) for i in range(2)]
            hT = [sb(s1, f"hT{i}", [128, 8, 512], BF16) for i in range(2)]
            stt = [sb(s1, f"stt{i}", [128, 8], F32) for i in range(2)]
            NF = 2
            qs = [sb(s1, f"qs{i}", [128, 512], F32) for i in range(NF)]
            sgm = [sb(s1, f"sgm{i}", [128, 512], F32) for i in range(NF)]
            lf = sb(s1, "lf0", [128, 512], F32)
            bcs = sb(s1, "bcs0", [128, 512], F32)
            eb = [sb(s1, f"eb{i}", [128, 512], F32) for i in range(NF)]
            enb = sb(s1, "enb0", [128, 512], F32)
            kk = sb(s1, "kk0", [128, 512], F32)
            tmpk = sb(s1, "tmpk0", [128, 512], F32)
            QA = [[sb(s1, f"QA{p}_{h}", [128, 512], BF16) for h in range(4)] for p in range(2)]
            QB = [[sb(s1, f"QB{p}_{h}", [128, 512], BF16) for h in range(4)] for p in range(2)]
            kt = [[sb(s1, f"kt{p}_{h}", [128, 512], BF16) for h in range(4)] for p in range(2)]
            kd = [[sb(s1, f"kd{p}_{h}", [128, 512], BF16) for h in range(4)] for p in range(2)]
            ebl = [[sb(s1, f"ebl{p}_{h}", [128, 8], F32) for h in range(4)] for p in range(2)]
            kdA = [sb(s1, f"kdA{h}", [128, 4, 128], BF16) for h in range(4)]
            kdB = [sb(s1, f"kdB{h}", [128, 4, 128], BF16) for h in range(4)]
            vtok = [sb(s1, f"vtok{i}", [128, 4, 512], BF16) for i in range(2)]
            sgg = [sb(s1, f"sgg{i}", [128, 4, 512], BF16) for i in range(2)]
            avst = sb(s1, "avst", [128, 4, 512], BF16)
            qst = sb(s1, "qst", [128, 4, 512], BF16)
            kst = sb(s1, "kst", [128, 4, 512], BF16)
            Sf = [[sb(s1, f"Sf{h}_{i}", [128, 128], F32) for i in range(2)] for h in range(4)]
            Sb = [[sb(s1, f"Sb{h}_{i}", [128, 128], BF16) for i in range(2)] for h in range(4)]
            ATs = [sb(s1, f"ATs{h}", [128, 128], BF16) for h in range(4)]
            hgo = [[sb(s1, f"hgo{q}_{h}", [128, 128], BF16) for h in range(4)] for q in range(2)]
            ost2 = [sb(s1, f"ost2_{q}", [128, 12], F32) for q in range(2)]
            osb = [sb(s1, f"osb{h}", [128, 128], F32) for h in range(4)]
            hgst = [sb(s1, f"hgst{i}", [128, 4, 512], BF16) for i in range(2)]
            psTr = ps(s1, "psTr", [128, 1024], BF16)
            psF = [ps(s1, f"psF{i}", [128, 512], F32) for i in range(2)]
            psH = [ps(s1, f"psH{h}", [128, 512], F32) for h in range(4)]
            psK = ps(s1, "psK", [128, 1024], BF16)

            for p in range(2):
                for h in range(4):
                    em.op(G, lambda h=h, p=p: nc.gpsimd.memset(QA[p][h][:], 0.0), w=[QA[p][h]])
                    em.op(G, lambda h=h, p=p: nc.gpsimd.memset(QB[p][h][:], 0.0), w=[QB[p][h]])
            for h in range(4):
                em.op(G, lambda h=h: nc.gpsimd.memset(kdA[h][:], 0.0), w=[kdA[h]])
                em.op(G, lambda h=h: nc.gpsimd.memset(kdB[h][:], 0.0), w=[kdB[h]])
                em.op(G, lambda h=h: nc.gpsimd.memset(Sf[h][0][:], 0.0), w=[Sf[h][0]])
                em.op(G, lambda h=h: nc.gpsimd.memset(Sb[h][0][:], 0.0), w=[Sb[h][0]])
            pf_i = [0]

            def next_psF():
                p = psF[pf_i[0] % 2]
                pf_i[0] += 1
                return p

            def load_x(t):
                em.dma(SP, xt[t % 2], xt[t % 2][:], x_d[t * 128:(t + 1) * 128, :], w=[xt[t % 2]])

            def Xa1(t):
                xb_, xnb, st_ = xt[t % 2], xn[t % 2], stt[t % 2]
                em.op(A, lambda: nc.scalar.activation(out=xnb[:], in_=xb_[:], func=AF.Square, accum_out=st_[:, 0:1]), r=[xb_], w=[xnb, st_])
                em.op(A, lambda: nc.scalar.activation(out=st_[:, 1:2], in_=st_[:, 0:1], func=AF.Ln, scale=1.0 / D, bias=cst[:, C_EPS:C_EPS + 1]), r=[st_, cst], w=[st_])
                em.op(A, lambda: nc.scalar.activation(out=st_[:, 2:3], in_=st_[:, 1:2], func=AF.Exp, scale=-0.5), r=[st_], w=[st_])

            def Xa2(t):
                xb_, xnb, st_ = xt[t % 2], xn[t % 2], stt[t % 2]
                em.op(V, lambda: nc.vector.tensor_scalar(out=xnb[:], in0=xb_[:], scalar1=st_[:, 2:3], scalar2=0.0, op0=ALU.mult, op1=ALU.add), r=[xb_, st_], w=[xnb])
                if t + 2 < NT:
                    load_x(t + 2)

            def Xa(t):
                Xa1(t)
                Xa2(t)

            def Xb(t):
                st, j = t // 4, t % 4
                hTc, xnb = hT[st % 2], xn[t % 2]
                for kc in range(8):
                    em.op(P, lambda kc=kc: nc.tensor.transpose(out=psTr[:, kc * 128:(kc + 1) * 128], in_=xnb[:, kc * 128:(kc + 1) * 128], identity=identb[:]),
                          r=[xnb, identb], w=[psTr], signal=(kc == 7))
                for kc in range(8):
                    o = hTc[:, kc, j * 128:(j + 1) * 128]
                    i_ = psTr[:, kc * 128:(kc + 1) * 128]
                    if t % 2 == 0:
                        em.op(V, lambda o=o, i_=i_, kc=kc: nc.vector.tensor_scalar(out=o, in0=i_, scalar1=cst[:, C_G1 + kc:C_G1 + kc + 1], scalar2=cst[:, C_S1 + kc:C_S1 + kc + 1], op0=ALU.mult, op1=ALU.add),
                              r=[psTr, cst], w=[hTc])
                    else:
                        em.op(A, lambda o=o, i_=i_, kc=kc: nc.scalar.activation(out=o, in_=i_, func=AF.Identity, scale=cst[:, C_G1 + kc:C_G1 + kc + 1], bias=cst[:, C_S1 + kc:C_S1 + kc + 1]),
                              r=[psTr, cst], w=[hTc])

            def proj_fm(hTc, c0):
                pf = next_psF()
                for kc in range(8):
                    em.op(P, lambda kc=kc: nc.tensor.matmul(pf[:, :], lhsT=win_bf[:, kc, c0:c0 + 128], rhs=hTc[:, kc, :], start=(kc == 0), stop=(kc == 7)),
                          r=[hTc, win_bf], w=[pf], signal=(kc == 7))
                return pf

            def EW(st, h, f):
                p = st % 2
                em.op(A, lambda: nc.scalar.activation(out=lf[:], in_=sgm[f][:], func=AF.Ln, scale=cst[:, C_OML + h:C_OML + h + 1], bias=cst[:, C_LB + h:C_LB + h + 1]), r=[sgm[f], cst], w=[lf])
                em.op(V, lambda: nc.vector.tensor_tensor_scan(out=bcs[:], data0=mscan[:], data1=lf[:], initial=0.0, op0=ALU.mult, op1=ALU.add), r=[mscan, lf], w=[bcs])
                em.op(A, lambda: nc.scalar.activation(out=eb[f][:], in_=bcs[:], func=AF.Exp), r=[bcs], w=[eb[f]])
                em.op(A, lambda: nc.scalar.activation(out=enb[:], in_=bcs[:], func=AF.Exp, scale=-1.0), r=[bcs], w=[enb])
                qv = qs[f][:].rearrange("p (t e c) -> p t e c", e=2, c=64)
                ev = eb[f][:].rearrange("p (t e c) -> p t e c", e=2, c=64)
                qav = QA[p][h][:].rearrange("p (t e c) -> p t e c", e=2, c=64)
                qbv = QB[p][h][:].rearrange("p (t e c) -> p t e c", e=2, c=64)
                em.op(G, lambda: nc.gpsimd.tensor_tensor(out=qav[:, :, 0, :], in0=qv[:, :, 0, :], in1=ev[:, :, 0, :], op=ALU.mult), r=[qs[f], eb[f]], w=[QA[p][h]])
                em.op(G, lambda: nc.gpsimd.tensor_tensor(out=qbv[:, :, 1, :], in0=qv[:, :, 1, :], in1=ev[:, :, 1, :], op=ALU.mult), r=[qs[f], eb[f]], w=[QB[p][h]])
                em.op(V, lambda: nc.vector.tensor_scalar(out=kk[:], in0=sgm[f][:], scalar1=cst[:, C_NOML + h:C_NOML + h + 1], scalar2=cst[:, C_OML + h:C_OML + h + 1], op0=ALU.mult, op1=ALU.add),
                      r=[sgm[f], cst], w=[kk])
                em.op(V, lambda: nc.vector.tensor_tensor(out=tmpk[:], in0=kk[:], in1=enb[:], op=ALU.mult), r=[kk, enb], w=[tmpk])
                em.op(G, lambda: nc.gpsimd.tensor_copy(out=kt[p][h][:], in_=tmpk[:]), r=[tmpk], w=[kt[p][h]])
                ebv = eb[f][:].rearrange("p (c t) -> p c t", t=64)
                em.op(V, lambda: nc.vector.tensor_tensor(out=kd[p][h][:].rearrange("p (c t) -> p c t", t=64), in0=tmpk[:].rearrange("p (c t) -> p c t", t=64),
                                                         in1=ebv[:, :, 63:64].to_broadcast([128, 8, 64]), op=ALU.mult), r=[tmpk, eb[f]], w=[kd[p][h]])
                em.op(G, lambda: nc.gpsimd.tensor_copy(out=ebl[p][h][:].rearrange("p (c o) -> p c o", o=1), in_=ebv[:, :, 63:64]), r=[eb[f]], w=[ebl[p][h]])

            def IP(st):
                hTc = hT[st % 2]
                vt, sg_t = vtok[st % 2], sgg[st % 2]
                for h in range(4):
                    f = (st * 4 + h) % NF
                    pq = proj_fm(hTc, h * 128)
                    em.op(A, lambda: nc.scalar.activation(out=qs[f][:], in_=pq[:, :], func=AF.Silu), r=[pq], w=[qs[f]])
                    pfg = proj_fm(hTc, 512 + h * 128)
                    em.op(A, lambda: nc.scalar.activation(out=sgm[f][:], in_=pfg[:, :], func=AF.Sigmoid), r=[pfg], w=[sgm[f]])
                    EW(st, h, f)
                for j in range(4):
                    for gi, c0 in enumerate((1024, 1536, 3072)):
                        pf = next_psF()
                        for kc in range(8):
                            em.op(P, lambda kc=kc, pf=pf, c0=c0, j=j: nc.tensor.matmul(pf[:, :], lhsT=hTc[:, kc, j * 128:(j + 1) * 128], rhs=win_bf[:, kc, c0:c0 + 512], start=(kc == 0), stop=(kc == 7)),
                                  r=[hTc, win_bf], w=[pf], signal=(kc == 7))
                        if gi == 0:
                            em.op(V, lambda pf=pf, j=j: nc.vector.tensor_copy(out=vt[:, j, :], in_=pf[:, :]), r=[pf], w=[vt])
                        elif gi == 1:
                            em.op(A, lambda pf=pf, j=j: nc.scalar.activation(out=sg_t[:, j, :], in_=pf[:, :], func=AF.Silu), r=[pf], w=[sg_t])
                        else:
                            em.op(V, lambda pf=pf, j=j: nc.vector.tensor_copy(out=avst[:, j, :], in_=pf[:, :]), r=[pf], w=[avst])
                em.dma(SP, avst, v_d[st * 512:(st + 1) * 512, :].rearrange("(j p) f -> p j f", p=128), avst[:], r=[avst], w=[v_B])
                for pr in range(4):
                    pf = proj_fm(hTc, 2048 + pr * 128)
                    em.op(A, lambda pf=pf, pr=pr: nc.scalar.mul(out=qst[:, pr, :], in_=pf[:, :], mul=0.125), r=[pf], w=[qst])
                    pf = proj_fm(hTc, 2560 + pr * 128)
                    em.op(V, lambda pf=pf, pr=pr: nc.vector.tensor_copy(out=kst[:, pr, :], in_=pf[:, :]), r=[pf], w=[kst])
                em.dma(SP, qst, qT_d.rearrange("(pr two) e t -> (two e) pr t", two=2)[:, :, st * 512:(st + 1) * 512], qst[:], r=[qst], w=[qT_B])
                em.dma(SP, kst, kT_d.rearrange("(pr two) e t -> (two e) pr t", two=2)[:, :, st * 512:(st + 1) * 512], kst[:], r=[kst], w=[kT_B])

            def REC(st, hooks):
                p = st % 2
                vt, sg_t, hg_t = vtok[p], sgg[p], hgst[p]
                for h in range(4):
                    for j in range(4):
                        em.op(P, lambda j=j, h=h: nc.tensor.transpose(out=psK[:, j * 128:(j + 1) * 128], in_=kd[p][h][:, j * 128:(j + 1) * 128], identity=identb[:]),
                              r=[kd[p][h], identb], w=[psK], signal=(j == 3))
                    em.op(V, lambda h=h: nc.vector.tensor_copy(out=kdA[h][0:64, :, :].rearrange("p j k -> p (j k)"), in_=psK[0:64, 0:512]), r=[psK], w=[kdA[h]])
                    em.op(V, lambda h=h: nc.vector.tensor_copy(out=kdB[h][64:128, :, :].rearrange("p j k -> p (j k)"), in_=psK[64:128, 0:512]), r=[psK], w=[kdB[h]])
                def emit_T(jj):
                    for h in range(4):
                        em.op(P, lambda h=h: nc.tensor.transpose(out=psK[:, 512 + h * 128:512 + (h + 1) * 128], in_=hgo[jj % 2][h][:], identity=identb[:]), r=[hgo[jj % 2][h], identb], w=[psK], signal=(h == 3))
                    em.op(V, lambda: nc.vector.tensor_copy(out=hg_t[:, :, jj * 128:(jj + 1) * 128], in_=psK[:, 512:1024].rearrange("p (h t) -> p h t", t=128)), r=[psK], w=[hg_t])

                for j in range(4):
                    if hooks[j][0]:
                        hooks[j][0]()
                    cs = slice(j * 128, (j + 1) * 128)
                    for h in range(4):
                        hc = slice(h * 128, (h + 1) * 128)
                        pH = psH[h]
                        em.op(P, lambda pH=pH, h=h: nc.tensor.matmul(pH[:, 0:128], lhsT=kt[p][h][:, cs], rhs=QA[p][h][:, cs], start=True, stop=False), r=[kt[p][h], QA[p][h]], w=[pH], signal=False)
                        em.op(P, lambda pH=pH, h=h: nc.tensor.matmul(pH[:, 0:128], lhsT=kt[p][h][:, cs], rhs=QB[p][h][:, cs], start=False, stop=True), r=[kt[p][h], QB[p][h]], w=[pH], signal=False)
                        em.op(P, lambda pH=pH, h=h, hc=hc: nc.tensor.matmul(pH[:, 256:384], lhsT=kdA[h][:, j, :], rhs=vt[:, j, hc], start=True, stop=True), r=[kdA[h], vt], w=[pH], signal=False)
                        em.op(P, lambda pH=pH, h=h, hc=hc: nc.tensor.matmul(pH[:, 384:512], lhsT=kdB[h][:, j, :], rhs=vt[:, j, hc], start=True, stop=True), r=[kdB[h], vt], w=[pH], signal=True)
                        em.op(V, lambda pH=pH, h=h: nc.vector.tensor_tensor(out=ATs[h][:], in0=pH[:, 0:128], in1=mhg[:], op=ALU.mult), r=[pH, mhg], w=[ATs[h]])
                    for h in range(4):
                        pH = psH[h]
                        em.op(V, lambda pH=pH, h=h: nc.vector.scalar_tensor_tensor(out=Sf[h][1][:], in0=Sf[h][0][:], scalar=ebl[p][h][:, 2 * j:2 * j + 1], in1=pH[:, 256:384], op0=ALU.mult, op1=ALU.add),
                              r=[Sf[h][0], ebl[p][h], pH], w=[Sf[h][1]])
                        em.op(G, lambda h=h: nc.gpsimd.tensor_copy(out=Sb[h][1][:], in_=Sf[h][1][:]), r=[Sf[h][1]], w=[Sb[h][1]])
                    if hooks[j][1]:
                        hooks[j][1]()
                    for h in range(4):
                        hc = slice(h * 128, (h + 1) * 128)
                        pH = psH[h]
                        em.op(P, lambda pH=pH, h=h: nc.tensor.matmul(pH[:, 128:256], lhsT=QA[p][h][:, cs], rhs=Sb[h][0][:], start=True, stop=False), r=[QA[p][h], Sb[h][0]], w=[pH], signal=False)
                        em.op(P, lambda pH=pH, h=h: nc.tensor.matmul(pH[:, 128:256], lhsT=QB[p][h][:, cs], rhs=Sb[h][1][:], start=False, stop=False), r=[QB[p][h], Sb[h][1]], w=[pH], signal=False)
                        em.op(P, lambda pH=pH, h=h, hc=hc: nc.tensor.matmul(pH[:, 128:256], lhsT=ATs[h][:], rhs=vt[:, j, hc], start=False, stop=True), r=[ATs[h], vt], w=[pH], signal=True)
                    if j > 0:
                        emit_T(j - 1)
                    o2 = ost2[j % 2]
                    for h in range(4):
                        pH = psH[h]
                        hg_ = hgo[j % 2][h]
                        em.op(V, lambda pH=pH, h=h: nc.vector.tensor_copy(out=osb[h][:], in_=pH[:, 128:256]), r=[pH], w=[osb[h]])
                        em.op(V, lambda pH=pH, h=h: nc.vector.scalar_tensor_tensor(out=Sf[h][0][:], in0=Sf[h][1][:], scalar=ebl[p][h][:, 2 * j + 1:2 * j + 2], in1=pH[:, 384:512], op0=ALU.mult, op1=ALU.add),
                              r=[Sf[h][1], ebl[p][h], pH], w=[Sf[h][0]])
                        em.op(G, lambda h=h: nc.gpsimd.tensor_copy(out=Sb[h][0][:], in_=Sf[h][0][:]), r=[Sf[h][0]], w=[Sb[h][0]])
                        em.op(A, lambda o2=o2, h=h, hg_=hg_: nc.scalar.activation(out=hg_[:], in_=osb[h][:], func=AF.Square, accum_out=o2[:, h:h + 1]), r=[osb[h]], w=[hg_, o2])
                    em.op(A, lambda o2=o2: nc.scalar.activation(out=o2[:, 4:8], in_=o2[:, 0:4], func=AF.Ln, scale=1.0 / 128, bias=cst[:, C_EPS:C_EPS + 1]), r=[o2, cst], w=[o2])
                    em.op(A, lambda o2=o2: nc.scalar.activation(out=o2[:, 8:12], in_=o2[:, 4:8], func=AF.Exp, scale=-0.5), r=[o2], w=[o2])
                    for h in range(4):
                        hc = slice(h * 128, (h + 1) * 128)
                        hg_ = hgo[j % 2][h]
                        em.op(V, lambda o2=o2, hg_=hg_, hc=hc, h=h: nc.vector.scalar_tensor_tensor(out=hg_[:], in0=osb[h][:], scalar=o2[:, 8 + h:9 + h], in1=sg_t[:, j, hc], op0=ALU.mult, op1=ALU.mult),
                              r=[osb[h], o2, sg_t], w=[hg_])
                    if hooks[j][2]:
                        hooks[j][2]()
                emit_T(3)
                for h in range(4):
                    em.dma(SP, hg_t, mix_d[st * 4:(st + 1) * 4, :, h, :].rearrange("j p t -> p j t"), hg_t[:, h, :].rearrange("p (j t) -> p j t", t=128), r=[hg_t], w=[mix_B])

            load_x(0)
            load_x(1)
            for t in range(4):
                Xa(t)
                Xb(t)
            IP(0)
            for t in range(4, 8):
                Xa(t)
                Xb(t)
            Xa(8)
            for st in range(NST):
                if st + 1 < NST:
                    IP(st + 1)
                hooks = []
                for j in range(4):
                    t = (st + 2) * 4 + j
                    if t < NT:
                        h0 = (lambda t=t: Xa1(t + 1)) if t + 1 < NT else None
                        h1 = (lambda t=t: Xb(t))
                        h2 = (lambda t=t: Xa2(t + 1)) if t + 1 < NT else None
                        hooks.append((h0, h1, h2))
                    else:
                        hooks.append((None, None, None))
                REC(st, hooks)
        em.barrier()

        wout_bf = sb(es, "wout_bf", [128, 8, D], BF16)
        wdnA = sb(es, "wdnA", [128, 11, D], BF16)

        with ExitStack() as s2:
            stage2 = [sb(s2, f"wstg2_{i}", [128, D], F32) for i in range(2)]
            wtasks = cast_tasks(wout_bf, wout_d, 8, D, stage2,
                                (lambda kc: (Vc[:, R_HG:R_HG + 1] if kc < 4 else Vc[:, R_AG + kc - 4:R_AG + kc - 3]), gate_bc[:, 0:D]))
            wtasks += cast_tasks(wdnA, wdn_d[0:11 * 128, :], 11, D, stage2, (None, gate_bc[:, D:2 * D]))
            QT = [sb(s2, f"QT{i}", [128, S], BF16) for i in range(2)]
            KT = [sb(s2, f"KT{i}", [128, S], BF16) for i in range(2)]
            for i in range(2):
                em.op(G, lambda i=i: nc.gpsimd.memset(QT[i][:], 0.0), w=[QT[i]])
            Vh = [[sb(s2, f"Vh{i}_{d}", [128, 32, 65], BF16) for d in range(3)] for i in range(2)]
            acc = sb(s2, "acc", [65, S], F32)
            attT = sb(s2, "attT", [128, 4, S], BF16)
            ssb = sb(s2, "ssb", [128, S], F32)
            PT = [sb(s2, f"PT{i}", [128, 512], BF16) for i in range(3)]
            rl = [sb(s2, f"rl{i}", [64, 512], F32) for i in range(2)]
            attf = [sb(s2, f"attf{i}", [64, 512], F32) for i in range(2)]
            sq = [sb(s2, f"sq{i}", [64, 512], BF16) for i in range(4)]
            rsd = [sb(s2, f"rsd{i}", [128, 512], F32) for i in range(2)]
            psS = [ps(s2, f"psS{i}", [128, 512], F32) for i in range(3)]
            psO = [ps(s2, f"psO{i}", [128, 512], F32) for i in range(4)]
            psL = ps(s2, "psL", [128, 512], F32)
            for i in range(2):
                for d in range(3):
                    em.op(G, lambda i=i, d=d: nc.gpsimd.memset(Vh[i][d][:, :, 64:65], 1.0), w=[Vh[i][d]])
            DIL = (1, 4, 16)

            def load_head(h):
                i = h % 2
                em.dma(SP, QT[i], QT[i][64 * i:64 * i + 64, :], qT_d[h], r=[qT_B], w=[QT[i]])
                if h % 2 == 0:
                    kp = (h // 2) % 2
                    em.dma(SP, KT[kp], KT[kp][:], kT_d[h:h + 2].rearrange("two e t -> (two e) t"), r=[kT_B], w=[KT[kp]])
                for di, d in enumerate(DIL):
                    nb = 32 // d
                    src = v_d[:, h * 64:(h + 1) * 64].rearrange("(n i r) e -> i r n e", i=128, r=d)
                    if d <= nb:
                        for r_ in range(d):
                            em.dma(SP, Vh[i][di], Vh[i][di][:, r_ * nb:(r_ + 1) * nb, 0:64], src[:, r_, :, :], r=[v_B], w=[Vh[i][di]])
                    else:
                        for n_ in range(nb):
                            em.dma(SP, Vh[i][di], Vh[i][di][:, n_:32:nb, 0:64], src[:, :, n_, :], r=[v_B], w=[Vh[i][di]])

            load_head(0)
            s_i = [0]
            deferred = []
            LAG = 2
            for h in range(8):
                i = h % 2
                if h + 1 < 8:
                    load_head(h + 1)
                pr, half = h // 2, h % 2
                Q, K = QT[i], KT[pr % 2]
                for R in range(2):
                    blocks = []
                    for n in range(16 * R, 16 * R + 16):
                        blocks.append((0, 1, 0, n))
                    for n in range(4 * R, 4 * R + 4):
                        for r_ in range(4):
                            blocks.append((1, 4, r_, n))
                    for r_ in range(16):
                        blocks.append((2, 16, r_, R))
                    pairs = [blocks[k:k + 2] for k in range(0, len(blocks), 2)]
                    touched = set()
                    pv_ops = []
                    for pi, pair in enumerate(pairs):
                        ops = []
                        for bi, (di, d, r_, n) in enumerate(pair):
                            nb = 32 // d
                            blk = r_ * nb + n
                            c0 = bi * 256
                            for half_, vb in ((0, blk - 1), (1, blk)):
                                if half_ == 0 and n == 0:
                                    continue
                                pc = c0 + half_ * 128
                                if d == 1:
                                    b_ = (n % 16) // 4
                                    ops.append((b_, slice((n % 4) * 128, (n % 4) * 128 + 128), di, vb, slice(pc, pc + 128)))
                                elif d == 4:
                                    b_ = n % 4
                                    ops.append((b_, slice(r_, r_ + 4 * 127 + 1, 4), di, vb, slice(pc, pc + 128)))
                                else:
                                    for b_ in range(4):
                                        ops.append((b_, slice(r_, r_ + 16 * 31 + 1, 16), di, vb, slice(pc + 32 * b_, pc + 32 * b_ + 32)))
                        pv_ops.append(ops)
                    last_touch = {}
                    for pi, ops in enumerate(pv_ops):
                        for oi, o_ in enumerate(ops):
                            last_touch[o_[0]] = (pi, oi)
                    stage_buf = {}

                    def emit_qk(pi, pairs=pairs, Q=Q, K=K, stage_buf=stage_buf):
                        pS = psS[s_i[0] % 3]
                        pt = PT[s_i[0] % 3]
                        s_i[0] += 1
                        stage_buf[pi] = pt
                        for bi, (di, d, r_, n) in enumerate(pairs[pi]):
                            q0 = n * 128 * d + r_
                            qsl = slice(q0, q0 + 127 * d + 1, d)
                            c0 = bi * 256
                            p0 = (n - 1) * 128 * d + r_ if n > 0 else q0
                            psl = slice(p0, p0 + 127 * d + 1, d)
                            em.op(P, lambda pS=pS, c0=c0, psl=psl, qsl=qsl: nc.tensor.matmul(pS[:, c0:c0 + 128], lhsT=K[:, psl], rhs=Q[:, qsl], start=True, stop=True), r=[K, Q], w=[pS], signal=False)
                            em.op(P, lambda pS=pS, c0=c0, qsl=qsl: nc.tensor.matmul(pS[:, c0 + 128:c0 + 256], lhsT=K[:, qsl], rhs=Q[:, qsl], start=True, stop=True), r=[K, Q], w=[pS], signal=(bi == 1))
                        em.op(A, lambda pS=pS, pt=pt: nc.scalar.activation(out=pt[:], in_=pS[:, :], func=AF.Exp), r=[pS], w=[pt])
                        em.op(V, lambda pt=pt: nc.vector.tensor_tensor(out=pt[:], in0=pt[:], in1=matt[:], op=ALU.mult), r=[pt, matt], w=[pt])

                    def emit_pv(pi, pv_ops=pv_ops, stage_buf=stage_buf, touched=touched, last_touch=last_touch, i=i):
                        pt = stage_buf[pi]
                        ops = pv_ops[pi]
                        for oi, (b_, osl, di, vb, psl) in enumerate(ops):
                            first = b_ not in touched
                            touched.add(b_)
                            last = last_touch[b_] == (pi, oi)
                            Vd = Vh[i][di]
                            em.op(P, lambda b_=b_, osl=osl, Vd=Vd, vb=vb, psl=psl, pt=pt, first=first, last=last: nc.tensor.matmul(psO[b_][0:65, osl], lhsT=Vd[:, vb, :], rhs=pt[:, psl], start=first, stop=last, skip_group_check=True),
                                  r=[Vd, pt], w=[psO[b_]], signal=(oi == len(ops) - 1))

                    npairs = len(pairs)
                    for idx in range(npairs + LAG):
                        if idx < npairs:
                            emit_qk(idx)
                        if idx >= LAG:
                            emit_pv(idx - LAG)
                        if deferred and idx % 2 == 1:
                            deferred.pop(0)()
                        if wtasks and idx % 8 == 4:
                            wtasks.pop(0)()
                    while deferred:
                        deferred.pop(0)()
                    for b_ in range(4):
                        csl = slice(R * 2048 + b_ * 512, R * 2048 + (b_ + 1) * 512)
                        if b_ % 2 == 0:
                            em.op(V, lambda b_=b_, csl=csl: nc.vector.tensor_copy(out=acc[0:65, csl], in_=psO[b_][0:65, :]), r=[psO[b_]], w=[acc])
                        else:
                            em.op(A, lambda b_=b_, csl=csl: nc.scalar.copy(out=acc[0:65, csl], in_=psO[b_][0:65, :]), r=[psO[b_]], w=[acc])
                    fin1s, fin2s = [], []
                    for b_ in range(4):
                        c = R * 4 + b_
                        csl = slice(c * 512, (c + 1) * 512)

                        def fin1(c=c, csl=csl, pr=pr, half=half, h=h):
                            rl_, af_, sq_ = rl[c % 2], attf[c % 2], sq[c % 4]
                            em.op(P, lambda: nc.tensor.matmul(psL[0:64, :], lhsT=onesf[64:65, 0:64], rhs=acc[64:65, csl], start=True, stop=True), r=[onesf, acc], w=[psL], signal=True)
                            em.op(A, lambda: nc.scalar.activation(out=rl_[:], in_=psL[0:64, :], func=AF.Ln), r=[psL], w=[rl_])
                            em.op(A, lambda: nc.scalar.activation(out=rl_[:], in_=rl_[:], func=AF.Exp, scale=-1.0), r=[rl_], w=[rl_])
                            em.op(V, lambda: nc.vector.tensor_tensor(out=af_[:], in0=acc[0:64, csl], in1=rl_[:], op=ALU.mult), r=[acc, rl_], w=[af_])
                            em.op(A, lambda: nc.scalar.copy(out=attT[64 * half:64 * half + 64, pr, csl], in_=af_[:]), r=[af_], w=[attT])
                            em.op(G, lambda: nc.gpsimd.tensor_tensor(out=sq_[:], in0=af_[:], in1=af_[:], op=ALU.mult), r=[af_], w=[sq_])

                        def fin2(c=c, csl=csl, h=h):
                            sq_ = sq[c % 4]
                            em.op(P, lambda: nc.tensor.matmul(psL[:, :], lhsT=onesb[0:64, :], rhs=sq_[:], start=True, stop=True), r=[onesb, sq_], w=[psL], signal=True)
                            if h == 0:
                                em.op(V, lambda: nc.vector.tensor_copy(out=ssb[:, csl], in_=psL[:, :]), r=[psL], w=[ssb])
                            else:
                                em.op(V, lambda: nc.vector.tensor_tensor(out=ssb[:, csl], in0=ssb[:, csl], in1=psL[:, :], op=ALU.add), r=[psL, ssb], w=[ssb])
                        fin1s.append(fin1)
                        fin2s.append(fin2)
                    deferred.extend(fin1s)
                    deferred.extend(fin2s)
            while deferred:
                deferred.pop(0)()
            while wtasks:
                wtasks.pop(0)()
            for c in range(8):
                csl = slice(c * 512, (c + 1) * 512)
                r_ = rsd[c % 2]
                em.op(A, lambda r_=r_, csl=csl: nc.scalar.activation(out=r_[:], in_=ssb[:, csl], func=AF.Ln, scale=1.0 / 512, bias=cst[:, C_EPS:C_EPS + 1]), r=[ssb, cst], w=[r_])
                em.op(A, lambda r_=r_: nc.scalar.activation(out=r_[:], in_=r_[:], func=AF.Exp, scale=-0.5), r=[r_], w=[r_])
                for pr in range(4):
                    e = V
                    eng = nc.vector
                    em.op(e, lambda r_=r_, csl=csl, pr=pr, eng=eng: eng.tensor_tensor(out=attT[:, pr, csl], in0=attT[:, pr, csl], in1=r_[:], op=ALU.mult), r=[attT, r_], w=[attT])
            for pr in range(4):
                em.dma(SP, attT, mix_d[:, :, 4 + pr, :].rearrange("j p t -> p j t"), attT[:, pr, :].rearrange("p (j t) -> p j t", t=128), r=[attT], w=[mix_B])
        em.barrier()

        with ExitStack() as s3:
            fg_bc = sb(s3, "fg_bc", [128, D], F32)
            em.dma(SP, fg_bc, fg_bc[:], fg_d.partition_broadcast(128), w=[fg_bc])
            wgu_bf = sb(s3, "wgu_bf", [128, 8, 2 * DFF], BF16)
            wdnB = sb(s3, "wdnB", [128, 11, D], BF16)
            with ExitStack() as s3a:
                stage = [sb(s3a, f"wstg3_{i}", [128, 2048], F32) for i in range(5)]
                cast_weight(wgu_bf, wgu_d, 8, 2 * DFF, stage)
                cast_weight(wdnB, wdn_d[11 * 128:22 * 128, :], 11, D, stage, fold=(None, gate_bc[:, D:2 * D]))
            em.barrier()

            xt3 = [sb(s3, f"xt3_{i}", [128, D], F32) for i in range(3)]
            mxt = [sb(s3, f"mxt{i}", [128, 8, 128], BF16) for i in range(2)]
            junk3 = sb(s3, "junk3", [128, D], BF16)
            xn2 = [sb(s3, "xn2_0", [128, D], BF16)] * 2
            h2T = [sb(s3, f"h2T{i}", [128, 8, 128], BF16) for i in range(2)]
            st3 = [sb(s3, f"st3_{i}", [128, 8], F32) for i in range(2)]
            sil = [sb(s3, f"sil{i}", [128, 512], F32) for i in range(2)]
            actb = [sb(s3, f"actb{i}", [128, 512], BF16) for i in range(2)]
            actT = [[sb(s3, f"actT{i}_{g}", [128, 4 if g < 5 else 2, 128], BF16) for g in range(6)] for i in range(2)]
            psX = ps(s3, "psX", [128, 1024], F32)
            psA = [ps(s3, f"psA{i}", [128, 512], F32) for i in range(2)]
            psU = [ps(s3, f"psU{i}", [128, 512], F32) for i in range(2)]
            psT3 = [ps(s3, f"psT3_{i}", [128, 1024], BF16) for i in range(2)]

            def load3(t):
                em.dma(SP, mxt[t % 2], mxt[t % 2][:], mix_d[t], r=[mix_B], w=[mxt[t % 2]])


            def loadx3(t):
                em.dma(SP, xt3[t % 3], xt3[t % 3][:], x_d[t * 128:(t + 1) * 128, :], r=[out_B] if False else [], w=[xt3[t % 3]])

            load3(0); loadx3(0); load3(1); loadx3(1); loadx3(2)
            tr_i = [0]

            def S1(t):
                xb_, mx = xt3[t % 3], mxt[t % 2]
                for nh in range(2):
                    for kc in range(8):
                        em.op(P, lambda nh=nh, kc=kc, mx=mx: nc.tensor.matmul(psX[:, nh * 512:(nh + 1) * 512], lhsT=mx[:, kc, :], rhs=wout_bf[:, kc, nh * 512:(nh + 1) * 512], start=(kc == 0), stop=(kc == 7)),
                              r=[mx, wout_bf], w=[psX], signal=(kc == 7 and nh == 1))
                em.op(V, lambda xb_=xb_: nc.vector.tensor_tensor(out=xb_[:], in0=xb_[:], in1=psX[:, :], op=ALU.add), r=[xb_, psX], w=[xb_])
                if t + 2 < NT:
                    load3(t + 2)
                s_ = st3[t % 2]
                em.op(A, lambda xb_=xb_, s_=s_: nc.scalar.activation(out=junk3[:], in_=xb_[:], func=AF.Square, accum_out=s_[:, 0:1]), r=[xb_], w=[junk3, s_])
                em.op(A, lambda s_=s_: nc.scalar.activation(out=s_[:, 1:2], in_=s_[:, 0:1], func=AF.Ln, scale=1.0 / D, bias=cst[:, C_EPS:C_EPS + 1]), r=[s_, cst], w=[s_])
                em.op(A, lambda s_=s_: nc.scalar.activation(out=s_[:, 2:3], in_=s_[:, 1:2], func=AF.Exp, scale=-0.5), r=[s_], w=[s_])
                xnb = xn2[t % 2]
                em.op(A, lambda xb_=xb_, xnb=xnb, s_=s_: nc.scalar.activation(out=xnb[:], in_=xb_[:], func=AF.Identity, scale=s_[:, 2:3]), r=[xb_, s_], w=[xnb])

            def S2(t):
                xnb = xn2[t % 2]
                pT = psT3[tr_i[0] % 2]; tr_i[0] += 1
                for kc in range(8):
                    em.op(P, lambda kc=kc, xnb=xnb, pT=pT: nc.tensor.transpose(out=pT[:, kc * 128:(kc + 1) * 128], in_=xnb[:, kc * 128:(kc + 1) * 128], identity=identb[:]), r=[xnb, identb], w=[pT], signal=(kc == 7))
                hb = h2T[t % 2]
                for kc in range(8):
                    o = hb[:, kc, :]
                    i_ = pT[:, kc * 128:(kc + 1) * 128]
                    if t % 2 == 0:
                        em.op(V, lambda o=o, i_=i_, kc=kc: nc.vector.tensor_scalar(out=o, in0=i_, scalar1=cst[:, C_G2 + kc:C_G2 + kc + 1], scalar2=cst[:, C_S2 + kc:C_S2 + kc + 1], op0=ALU.mult, op1=ALU.add), r=[pT, cst], w=[hb])
                    else:
                        em.op(A, lambda o=o, i_=i_, kc=kc: nc.scalar.activation(out=o, in_=i_, func=AF.Identity, scale=cst[:, C_G2 + kc:C_G2 + kc + 1], bias=cst[:, C_S2 + kc:C_S2 + kc + 1]), r=[pT, cst], w=[hb])

            def S3g(t, g):
                hb = h2T[t % 2]
                aT = actT[t % 2]
                c0 = g * 512
                w_ = min(512, DFF - c0)
                pa, pu = psA[g % 2], psU[g % 2]
                for kc in range(8):
                    em.op(P, lambda kc=kc, pa=pa, c0=c0, w_=w_, hb=hb: nc.tensor.matmul(pa[:, 0:w_], lhsT=hb[:, kc, :], rhs=wgu_bf[:, kc, c0:c0 + w_], start=(kc == 0), stop=(kc == 7)), r=[hb, wgu_bf], w=[pa], signal=(kc == 7))
                for kc in range(8):
                    em.op(P, lambda kc=kc, pu=pu, c0=c0, w_=w_, hb=hb: nc.tensor.matmul(pu[:, 0:w_], lhsT=hb[:, kc, :], rhs=wgu_bf[:, kc, DFF + c0:DFF + c0 + w_], start=(kc == 0), stop=(kc == 7)), r=[hb, wgu_bf], w=[pu], signal=(kc == 7))
                sl_, ab_ = sil[g % 2], actb[g % 2]
                em.op(A, lambda pa=pa, sl_=sl_, w_=w_: nc.scalar.activation(out=sl_[:, 0:w_], in_=pa[:, 0:w_], func=AF.Silu), r=[pa], w=[sl_])
                em.op(V, lambda pu=pu, sl_=sl_, ab_=ab_, w_=w_: nc.vector.tensor_tensor(out=ab_[:, 0:w_], in0=sl_[:, 0:w_], in1=pu[:, 0:w_], op=ALU.mult), r=[sl_, pu], w=[ab_])

            def S3t(t, g):
                aT = actT[t % 2]
                c0 = g * 512
                w_ = min(512, DFF - c0)
                ab_ = actb[g % 2]
                pT = psT3[tr_i[0] % 2]; tr_i[0] += 1
                nck = w_ // 128
                for q in range(nck):
                    em.op(P, lambda q=q, ab_=ab_, pT=pT: nc.tensor.transpose(out=pT[:, q * 128:(q + 1) * 128], in_=ab_[:, q * 128:(q + 1) * 128], identity=identb[:]), r=[ab_, identb], w=[pT], signal=(q == nck - 1))
                if g % 2 == 0:
                    em.op(A, lambda pT=pT, aT=aT, g=g, nck=nck: nc.scalar.copy(out=aT[g][:, :, :].rearrange("p a b -> p (a b)"), in_=pT[:, 0:nck * 128]), r=[pT], w=[aT[g]])
                else:
                    em.op(V, lambda pT=pT, aT=aT, g=g, nck=nck: nc.vector.tensor_copy(out=aT[g][:, :, :].rearrange("p a b -> p (a b)"), in_=pT[:, 0:nck * 128]), r=[pT], w=[aT[g]])

            def S4(t):
                xb_ = xt3[t % 3]
                aT = actT[t % 2]
                s_ = st3[t % 2]
                for nh in range(2):
                    for kc in range(22):
                        em.op(P, lambda nh=nh, kc=kc, aT=aT: nc.tensor.matmul(psX[:, nh * 512:(nh + 1) * 512], lhsT=aT[kc // 4][:, kc % 4, :], rhs=(wdnA if kc < 11 else wdnB)[:, kc % 11, nh * 512:(nh + 1) * 512], start=(kc == 0), stop=(kc == 21)),
                              r=[aT[kc // 4], wdnA, wdnB], w=[psX], signal=(kc == 21 and nh == 1))
                em.op(V, lambda xb_=xb_: nc.vector.tensor_tensor(out=xb_[:], in0=xb_[:], in1=psX[:, :], op=ALU.add), r=[xb_, psX], w=[xb_])
                em.op(A, lambda xb_=xb_, s_=s_: nc.scalar.activation(out=junk3[:], in_=xb_[:], func=AF.Square, accum_out=s_[:, 4:5]), r=[xb_], w=[junk3, s_])
                em.op(A, lambda s_=s_: nc.scalar.activation(out=s_[:, 5:6], in_=s_[:, 4:5], func=AF.Ln, scale=1.0 / D, bias=cst[:, C_EPS:C_EPS + 1]), r=[s_, cst], w=[s_])
                em.op(A, lambda s_=s_: nc.scalar.activation(out=s_[:, 6:7], in_=s_[:, 5:6], func=AF.Exp, scale=-0.5), r=[s_], w=[s_])
                em.op(V, lambda xb_=xb_, s_=s_: nc.vector.scalar_tensor_tensor(out=xb_[:], in0=xb_[:], scalar=s_[:, 6:7], in1=fg_bc[:], op0=ALU.mult, op1=ALU.mult), r=[xb_, s_, fg_bc], w=[xb_])
                em.dma(SP, xb_, out_d[t * 128:(t + 1) * 128, :], xb_[:], r=[xb_])

            S1(0)
            S2(0)
            for t in range(NT):
                if t >= 1 and t + 2 < NT:
                    loadx3(t + 2)
                S3g(t, 0)
                if t + 1 < NT:
                    S1(t + 1)
                for g in range(1, 6):
                    S3g(t, g)
                    S3t(t, g - 1)
                if t + 1 < NT:
                    S2(t + 1)
                S3t(t, 5)
                S4(t)
        em.finish()
        print(f"[build] instructions={em.nins} waits={em.nwaits}")
    return nc


def _consts():
    ident = np.eye(128, dtype=np.float32)
    s = np.arange(128)[:, None]
    t = np.arange(128)[None, :]
    mask_hg = ((s <= t) & ((s // 64) == (t // 64))).astype(np.float32)
    prev = (s >= t).astype(np.float32)
    cur = (s <= t).astype(np.float32)
    mask_att = np.concatenate([prev, cur, prev, cur], axis=1).astype(np.float32)
    mask_scan = np.ones((128, 512), np.float32)
    mask_scan[:, ::64] = 0.0
    return ident, mask_hg, mask_att, mask_scan


_NC_CACHE = {}


def kernel(x, c, w_ada, b_ada, norm1_g, w_in, hg_lb_logits, hg_onorm_g, att_onorm_g,
           w_out, norm2_g, w_gate_up, w_down, final_g):
    f32 = lambda a: np.ascontiguousarray(np.asarray(a, dtype=np.float32))
    x = f32(x); c = f32(c)
    B = x.shape[0]
    ident, mask_hg, mask_att, mask_scan = _consts()
    if "nc" not in _NC_CACHE:
        _NC_CACHE["nc"] = build_nc()
    nc = _NC_CACHE["nc"]
    shared = {
        "b_ada": f32(b_ada).reshape(1, 6 * D),
        "w_ada": f32(w_ada).reshape(D, 6 * D),
        "w_in": f32(w_in).reshape(D, 3584),
        "w_out": f32(w_out).reshape(D, D),
        "w_gu": f32(w_gate_up).reshape(D, 2 * DFF),
        "w_down": f32(w_down).reshape(DFF, D),
        "final_g": f32(final_g).reshape(1, D),
        "ident": ident, "mask_hg": mask_hg, "mask_att": mask_att, "mask_scan": mask_scan,
    }
    in_maps = []
    for b in range(B):
        vecs = np.concatenate([
            c[b].reshape(8, 128),
            f32(b_ada).reshape(48, 128),
            f32(norm1_g).reshape(8, 128),
            f32(norm2_g).reshape(8, 128),
            f32(hg_lb_logits).reshape(8, 128),
            f32(att_onorm_g).reshape(4, 128),
            f32(hg_onorm_g).reshape(1, 128),
        ], axis=0)
        m = dict(shared)
        m["x"] = x[b]
        m["vecs"] = np.ascontiguousarray(vecs)
        in_maps.append(m)
    res = run_bass_kernel_spmd(nc, in_maps, core_ids=list(range(B)))
    out = np.stack([np.asarray(r["out"], dtype=np.float32) for r in res.results], axis=0)
    return out
```

```python
import numpy as np
from contextlib import ExitStack
import concourse.bass as bass
import concourse.mybir as mybir
from concourse.alu_op_type import AluOpType as ALU
from concourse.bass_utils import run_bass_kernel_spmd

F32 = mybir.dt.float32
BF16 = mybir.dt.bfloat16
AF = mybir.ActivationFunctionType

S = 4096
D = 1024
NT = S // 128
NST = S // 512
DFF = 2816
EPS = 1e-6


class Buf:
    __slots__ = ("t", "lw", "rd", "name", "dsem", "wrd")

    def __init__(self, t, name=""):
        self.t = t
        self.lw = None
        self.rd = {}
        self.name = name
        self.dsem = None
        self.wrd = {}

    def __getitem__(self, idx):
        return self.t[idx]


class Dep:
    __slots__ = ("sem", "val", "eng", "idx")

    def __init__(self, sem, val, eng, idx):
        self.sem = sem; self.val = val; self.eng = eng; self.idx = idx


class Em:
    def __init__(self, nc, es):
        self.nc = nc
        self.E = {"pe": nc.tensor, "act": nc.scalar, "dve": nc.vector, "pool": nc.gpsimd, "sp": nc.sync}
        self.sem = {}; self.cnt = {}; self.issued = {}; self.waited = {}
        for k in self.E:
            self.sem[k] = es.enter_context(nc.semaphore("sem_" + k))
            self.cnt[k] = 0; self.issued[k] = 0; self.waited[k] = {}
        self.es = es
        self.dma_sems = []
        self.nwaits = 0
        self.nins = 0

    def new_dma_sem(self, name):
        s = self.es.enter_context(self.nc.semaphore(name))
        d = {"sem": s, "val": 0}
        self.dma_sems.append(d)
        return d

    def _wait(self, eng, dep):
        if dep is None:
            return
        key = id(dep.sem)
        if dep.eng == eng:
            if eng == "pe":
                return
        w = self.waited[eng]
        if w.get(key, -1) >= dep.val:
            return
        if dep.eng != "dma":
            assert self.cnt[dep.eng] >= dep.val, (
                f"dep on unsignaled instr: {dep.eng} cnt={self.cnt[dep.eng]} need={dep.val} (consumer {eng})")
        self.E[eng].wait_ge(dep.sem, dep.val)
        self.nwaits += 1
        w[key] = dep.val

    def op(self, eng, fn, r=(), w=(), signal=True):
        for b in r:
            self._wait(eng, b.lw)
        for b in w:
            self._wait(eng, b.lw)
            for d in b.rd.values():
                self._wait(eng, d)
        ins = fn()
        self.nins += 1
        self.issued[eng] += 1
        tag = self.cnt[eng] + 1
        if signal:
            ins.then_inc(self.sem[eng], 1)
            self.cnt[eng] = tag
        dep = Dep(self.sem[eng], tag, eng, self.issued[eng])
        for b in r:
            b.rd[id(dep.sem)] = dep
        for b in w:
            b.lw = dep
            b.rd = {}
        return ins

    def dma(self, q, dsem, out, in_, r=(), w=(), nowaw=False, **kw):
        if isinstance(dsem, Buf):
            if dsem.dsem is None:
                dsem.dsem = self.new_dma_sem("d_" + dsem.name)
            dsem = dsem.dsem
        for b in r:
            self._wait(q, b.lw)
        for b in w:
            if nowaw and b.lw is not None and b.lw.eng == "dma" and b.lw.sem is dsem["sem"]:
                for d in b.wrd.values():
                    self._wait(q, d)
            else:
                self._wait(q, b.lw)
                b.wrd = dict(b.rd)
            for d in b.rd.values():
                self._wait(q, d)
        ins = self.E[q].dma_start(out=out, in_=in_, **kw)
        ins.then_inc(dsem["sem"], 16)
        self.nins += 1
        dsem["val"] += 16
        dep = Dep(dsem["sem"], dsem["val"], "dma", None)
        for b in r:
            b.rd[id(dep.sem)] = dep
        for b in w:
            b.lw = dep
            b.rd = {}
        return dep

    def barrier(self):
        for e in ("pe", "act", "dve", "pool"):
            for f in ("pe", "act", "dve", "pool"):
                if e == f or self.cnt[f] == 0:
                    continue
                if self.waited[e].get(id(self.sem[f]), -1) < self.cnt[f]:
                    self.E[e].wait_ge(self.sem[f], self.cnt[f])
                    self.waited[e][id(self.sem[f])] = self.cnt[f]
        for e in ("sp", "pool", "act", "dve", "pe"):
            for d in self.dma_sems:
                if d["val"] > 0 and self.waited[e].get(id(d["sem"]), -1) < d["val"]:
                    self.E[e].wait_ge(d["sem"], d["val"])
                    self.waited[e][id(d["sem"])] = d["val"]
        for f in ("pe", "act", "dve", "pool"):
            if self.cnt[f] and self.waited["sp"].get(id(self.sem[f]), -1) < self.cnt[f]:
                self.E["sp"].wait_ge(self.sem[f], self.cnt[f])
                self.waited["sp"][id(self.sem[f])] = self.cnt[f]

    def finish(self):
        for d in self.dma_sems:
            if d["val"] > 0:
                self.E["sp"].wait_ge(d["sem"], d["val"])


R_C, R_BA, R_N1, R_N2, R_LB, R_AG, R_HG = 0, 8, 56, 64, 72, 80, 84
NVR = 85


def build_nc(debug=False):
    nc = bass.Bass("TRN2", target_bir_lowering=False)

    def din(name, shape, dt=F32):
        return nc.dram_tensor(name, shape, dt, kind="ExternalInput").ap()

    x_d = din("x", [S, D])
    vecs_d = din("vecs", [NVR, 128])
    bada_d = din("b_ada", [1, 6 * D])
    wada_d = din("w_ada", [D, 6 * D])
    win_d = din("w_in", [D, 3584])
    wout_d = din("w_out", [D, D])
    wgu_d = din("w_gu", [D, 2 * DFF])
    wdn_d = din("w_down", [DFF, D])
    fg_d = din("final_g", [1, D])
    ident_d = din("ident", [128, 128])
    mhg_d = din("mask_hg", [128, 128])
    matt_d = din("mask_att", [128, 512])
    mscan_d = din("mask_scan", [128, 512])
    out_d = nc.dram_tensor("out", [S, D], F32, kind="ExternalOutput").ap()

    skind = "ExternalOutput" if debug else "Internal"
    qT_d = nc.dram_tensor("qT_scr", [8, 64, S], BF16, kind=skind).ap()
    kT_d = nc.dram_tensor("kT_scr", [8, 64, S], BF16, kind=skind).ap()
    v_d = nc.dram_tensor("v_scr", [S, 512], BF16, kind=skind).ap()
    mix_d = nc.dram_tensor("mix_scr", [NT, 128, 8, 128], BF16, kind=skind).ap()

    with ExitStack() as es:
        em = Em(nc, es)
        V, A, P, G, SP = "dve", "act", "pe", "pool", "sp"

        def sb(scope, name, shape, dt):
            return Buf(scope.enter_context(nc.sbuf_tensor(name, shape, dt)), name)

        def ps(scope, name, shape, dt):
            return Buf(scope.enter_context(nc.psum_tensor(name, shape, dt)), name)

        qT_B, kT_B, v_B, mix_B = Buf(qT_d), Buf(kT_d), Buf(v_d), Buf(mix_d)
        out_B = Buf(out_d)
        identf = sb(es, "identf", [128, 128], F32)
        identb = sb(es, "identb", [128, 128], BF16)
        mhg = sb(es, "mhg", [128, 128], F32)
        matt = sb(es, "matt", [128, 512], BF16)
        mscan = sb(es, "mscan", [128, 512], F32)
        Vc = sb(es, "Vc", [128, 96], F32)
        cst = sb(es, "cst", [128, 64], F32)
        onesf = sb(es, "onesf", [128, 128], F32)
        onesb = sb(es, "onesb", [128, 128], BF16)
        gate_bc = sb(es, "gate_bc", [128, 2 * D], F32)
        C_CACT, C_LB, C_OML, C_NOML, C_G1, C_S1, C_G2, C_S2, C_EPS = 0, 8, 12, 16, 20, 28, 36, 44, 52

        em.dma(SP, identf, identf[:], ident_d, w=[identf])
        em.dma(SP, mhg, mhg[:], mhg_d, w=[mhg])
        em.dma(SP, mscan, mscan[:], mscan_d, w=[mscan])
        em.op(V, lambda: nc.vector.tensor_copy(out=identb[:], in_=identf[:]), r=[identf], w=[identb])
        em.op(G, lambda: nc.gpsimd.memset(onesf[:], 1.0), w=[onesf])
        em.op(G, lambda: nc.gpsimd.memset(onesb[:], 1.0), w=[onesb])
        em.op(G, lambda: nc.gpsimd.memset(cst[:, C_EPS:C_EPS + 1], EPS), w=[cst])

        s1 = ExitStack()
        win_bf = sb(s1, "win_bf", [128, 8, 3584], BF16)
        s1a = ExitStack()
        stage_in = [sb(s1a, f"wstg{i}", [128, 2048], F32) for i in range(2)]
        win_tasks = []
        _wi = [0]
        for kc_ in range(8):
            for (c0_, w__) in ((0, 2048), (2048, 1536)):
                def _t(kc_=kc_, c0_=c0_, w__=w__):
                    st_ = stage_in[_wi[0] % 2]
                    e = (V, A)[_wi[0] % 2]
                    _wi[0] += 1
                    em.dma(SP, st_, st_[:, 0:w__], win_d[kc_ * 128:(kc_ + 1) * 128, c0_:c0_ + w__], w=[st_])
                    o = win_bf[:, kc_, c0_:c0_ + w__]
                    if e == V:
                        em.op(V, lambda: nc.vector.tensor_copy(out=o, in_=st_[:, 0:w__]), r=[st_], w=[win_bf])
                    else:
                        em.op(A, lambda: nc.scalar.copy(out=o, in_=st_[:, 0:w__]), r=[st_], w=[win_bf])
                win_tasks.append(_t)
        with ExitStack() as sa:
            vec_sb = sb(sa, "vec_sb", [128, 128], F32)
            mattf = sb(sa, "mattf", [128, 512], F32)
            em.dma(SP, mattf, mattf[:], matt_d, w=[mattf])
            em.op(V, lambda: nc.vector.tensor_copy(out=matt[:], in_=mattf[:]), r=[mattf], w=[matt])
            bada_bc = sb(sa, "bada_bc", [128, 6 * D], F32)
            mod_bc = sb(sa, "mod_bc", [128, 6 * D], F32)
            cbc = sb(sa, "cbc", [128, 8, 128], F32)
            stg = [sb(sa, f"wada_stg{i}", [128, 8, 1024], F32) for i in range(2)]
            psT = ps(sa, "psT", [128, 512], F32)
            psM = [ps(sa, f"psM{i}", [128, 512], F32) for i in range(2)]

            em.op(G, lambda: nc.gpsimd.memset(vec_sb[:], 0.0), w=[vec_sb])
            em.dma(SP, vec_sb, vec_sb[0:NVR, :], vecs_d, w=[vec_sb])
            em.dma(SP, bada_bc, bada_bc[:], bada_d.partition_broadcast(128), w=[bada_bc])
            wv = wada_d.rearrange("(kc p) n -> p kc n", p=128)
            em.dma(SP, stg[0], stg[0][:], wv[:, :, 0:1024], w=[stg[0]])
            em.dma(SP, stg[1], stg[1][:], wv[:, :, 1024:2048], w=[stg[1]])
            em.op(P, lambda: nc.tensor.transpose(out=psT[:, 0:128], in_=vec_sb[:], identity=identf[:]), r=[vec_sb, identf], w=[psT])
            em.op(V, lambda: nc.vector.tensor_copy(out=Vc[:, 0:96], in_=psT[:, 0:96]), r=[psT], w=[Vc])
            em.op(A, lambda: nc.scalar.activation(out=cst[:, C_CACT:C_CACT + 8], in_=Vc[:, R_C:R_C + 8], func=AF.Silu), r=[Vc], w=[cst])
            em.op(V, lambda: nc.vector.tensor_sub(out=cst[:, 56:60], in0=Vc[:, R_LB:R_LB + 4], in1=Vc[:, R_LB + 4:R_LB + 8]), r=[Vc], w=[cst])
            em.op(A, lambda: nc.scalar.activation(out=cst[:, C_LB:C_LB + 4], in_=cst[:, 56:60], func=AF.Sigmoid), r=[cst], w=[cst])
            em.op(A, lambda: nc.scalar.activation(out=cst[:, C_OML:C_OML + 4], in_=cst[:, 56:60], func=AF.Sigmoid, scale=-1.0), r=[cst], w=[cst])
            em.op(V, lambda: nc.vector.tensor_scalar(out=cst[:, C_NOML:C_NOML + 4], in0=cst[:, C_OML:C_OML + 4], scalar1=-1.0, scalar2=0.0, op0=ALU.mult, op1=ALU.add), r=[cst], w=[cst])
            for kc in range(8):
                em.op(V, lambda kc=kc: nc.vector.tensor_copy(out=cbc[:, kc, :], in_=cst[:, C_CACT + kc:C_CACT + kc + 1].to_broadcast([128, 128])), r=[cst], w=[cbc])
            for blk in range(6):
                for _ in range(3):
                    if win_tasks:
                        win_tasks.pop(0)()
                sg_ = stg[blk % 2]
                for nh in range(2):
                    pm = psM[nh]
                    for kc in range(8):
                        em.op(P, lambda kc=kc, nh=nh, sg_=sg_, pm=pm: nc.tensor.matmul(pm[:, :], lhsT=cbc[:, kc, :], rhs=sg_[:, kc, nh * 512:(nh + 1) * 512], start=(kc == 0), stop=(kc == 7)),
                              r=[cbc, sg_], w=[pm], signal=(kc == 7))
                    c0 = blk * 1024 + nh * 512
                    em.op(V, lambda c0=c0, pm=pm: nc.vector.tensor_tensor(out=mod_bc[:, c0:c0 + 512], in0=pm[:, :], in1=bada_bc[:, c0:c0 + 512], op=ALU.add), r=[pm, bada_bc], w=[mod_bc])
                if blk + 2 < 6:
                    em.dma(SP, sg_, sg_[:], wv[:, :, (blk + 2) * 1024:(blk + 3) * 1024], w=[sg_])
            em.op(V, lambda: nc.vector.tensor_tensor(out=bada_bc[:].rearrange("p (j q) -> p j q", q=128), in0=mod_bc[:].rearrange("p (j q) -> p j q", q=128),
                                                     in1=identf[:].rearrange("p (o q) -> p o q", o=1).to_broadcast([128, 48, 128]), op=ALU.mult), r=[mod_bc, identf], w=[bada_bc])
            modT = sb(sa, "modT", [128, 48], F32)
            em.op(V, lambda: nc.vector.reduce_sum(out=modT[:], in_=bada_bc[:].rearrange("p (j q) -> p j q", q=128), axis=mybir.AxisListType.X), r=[bada_bc], w=[modT])
            em.op(V, lambda: nc.vector.scalar_tensor_tensor(out=cst[:, C_G1:C_G1 + 8], in0=modT[:, 8:16], scalar=1.0, in1=Vc[:, R_N1:R_N1 + 8], op0=ALU.add, op1=ALU.mult), r=[modT, Vc], w=[cst])
            em.op(V, lambda: nc.vector.tensor_copy(out=cst[:, C_S1:C_S1 + 8], in_=modT[:, 0:8]), r=[modT], w=[cst])
            em.op(V, lambda: nc.vector.scalar_tensor_tensor(out=cst[:, C_G2:C_G2 + 8], in0=modT[:, 32:40], scalar=1.0, in1=Vc[:, R_N2:R_N2 + 8], op0=ALU.add, op1=ALU.mult), r=[modT, Vc], w=[cst])
            em.op(V, lambda: nc.vector.tensor_copy(out=cst[:, C_S2:C_S2 + 8], in_=modT[:, 24:32]), r=[modT], w=[cst])
            em.op(V, lambda: nc.vector.tensor_copy(out=gate_bc[:, 0:D], in_=mod_bc[:, 2 * D:3 * D]), r=[mod_bc], w=[gate_bc])
            em.op(V, lambda: nc.vector.tensor_copy(out=gate_bc[:, D:2 * D], in_=mod_bc[:, 5 * D:6 * D]), r=[mod_bc], w=[gate_bc])
        em.barrier()

        cast_rr = [0]

        def cast_weight(dst, src_d, KC, N, stage, fold=None):
            pieces = []
            c = 0
            while c < N:
                w_ = min(2048, N - c)
                pieces.append((c, w_))
                c += w_
            for kc in range(KC):
                for (c0, w_) in pieces:
                    i = cast_rr[0] % 3
                    cast_rr[0] += 1
                    st_ = stage[i]
                    em.dma(SP, st_, st_[:, 0:w_], src_d[kc * 128:(kc + 1) * 128, c0:c0 + w_], w=[st_])
                    o = dst[:, kc, c0:c0 + w_]
                    if fold is None:
                        e = (V, A)[cast_rr[0] % 2]
                        if e == V:
                            em.op(V, lambda o=o, st_=st_, w_=w_: nc.vector.tensor_copy(out=o, in_=st_[:, 0:w_]), r=[st_], w=[dst])
                        elif e == A:
                            em.op(A, lambda o=o, st_=st_, w_=w_: nc.scalar.copy(out=o, in_=st_[:, 0:w_]), r=[st_], w=[dst])
                        else:
                            em.op(G, lambda o=o, st_=st_, w_=w_: nc.gpsimd.tensor_copy(out=o, in_=st_[:, 0:w_]), r=[st_], w=[dst])
                    else:
                        rowcol, colbc = fold
                        e = V
                        eng = nc.vector
                        if rowcol is not None:
                            rc = rowcol(kc)
                            em.op(e, lambda o=o, st_=st_, w_=w_, c0=c0, rc=rc, eng=eng: eng.scalar_tensor_tensor(out=o, in0=st_[:, 0:w_], scalar=rc, in1=colbc[:, c0:c0 + w_], op0=ALU.mult, op1=ALU.mult),
                                  r=[st_, gate_bc, Vc], w=[dst])
                        else:
                            em.op(e, lambda o=o, st_=st_, w_=w_, c0=c0, eng=eng: eng.tensor_tensor(out=o, in0=st_[:, 0:w_], in1=colbc[:, c0:c0 + w_], op=ALU.mult),
                                  r=[st_, gate_bc], w=[dst])

        def cast_tasks(dst, src_d, KC, N, stage, fold):
            tasks = []
            for kc in range(KC):
                def task(kc=kc):
                    st_ = stage[kc % len(stage)]
                    em.dma(SP, st_, st_[:, 0:N], src_d[kc * 128:(kc + 1) * 128, 0:N], w=[st_])
                    o = dst[:, kc, :]
                    rowcol, colbc = fold
                    if rowcol is not None:
                        rc = rowcol(kc)
                        em.op(V, lambda: nc.vector.scalar_tensor_tensor(out=o, in0=st_[:, 0:N], scalar=rc, in1=colbc[:, 0:N], op0=ALU.mult, op1=ALU.mult), r=[st_, gate_bc, Vc], w=[dst])
                    else:
                        em.op(V, lambda: nc.vector.tensor_tensor(out=o, in0=st_[:, 0:N], in1=colbc[:, 0:N], op=ALU.mult), r=[st_, gate_bc], w=[dst])
                tasks.append(task)
            return tasks

        while win_tasks:
            win_tasks.pop(0)()
        s1a.close()
        em.barrier()
        with s1:

            xt = [sb(s1, f"xt{i}", [128, D], F32) for i in range(2)]
            xn = [sb(s1, f"xn{i}", [128, D], BF16) for i in range(2)]
            hT = [sb(s1, f"hT{i}", [128, 8, 512], BF16) for i in range(2)]
            stt = [sb(s1, f"stt{i}", [128, 8], F32) for i in range(2)]
            NF = 2
            qs = [sb(s1, f"qs{i}", [128, 512], F32) for i in range(NF)]
            sgm = [sb(s1, f"sgm{i}", [128, 512], F32) for i in range(NF)]
            lf = sb(s1, "lf0", [128, 512], F32)
            bcs = sb(s1, "bcs0", [128, 512], F32)
            eb = [sb(s1, f"eb{i}", [128, 512], F32) for i in range(NF)]
            enb = sb(s1, "enb0", [128, 512], F32)
            kk = sb(s1, "kk0", [128, 512], F32)
            tmpk = sb(s1, "tmpk0", [128, 512], F32)
            QA = [[sb(s1, f"QA{p}_{h}", [128, 512], BF16) for h in range(4)] for p in range(2)]
            QB = [[sb(s1, f"QB{p}_{h}", [128, 512], BF16) for h in range(4)] for p in range(2)]
            kt = [[sb(s1, f"kt{p}_{h}", [128, 512], BF16) for h in range(4)] for p in range(2)]
            kd = [[sb(s1, f"kd{p}_{h}", [128, 512], BF16) for h in range(4)] for p in range(2)]
            ebl = [[sb(s1, f"ebl{p}_{h}", [128, 8], F32) for h in range(4)] for p in range(2)]
            kdA = [sb(s1, f"kdA{h}", [128, 4, 128], BF16) for h in range(4)]
            kdB = [sb(s1, f"kdB{h}", [128, 4, 128], BF16) for h in range(4)]
            vtok = [sb(s1, f"vtok{i}", [128, 4, 512], BF16) for i in range(2)]
            sgg = [sb(s1, f"sgg{i}", [128, 4, 512], BF16) for i in range(2)]
            avst = sb(s1, "avst", [128, 4, 512], BF16)
            qst = sb(s1, "qst", [128, 4, 512], BF16)
            kst = sb(s1, "kst", [128, 4, 512], BF16)
            Sf = [[sb(s1, f"Sf{h}_{i}", [128, 128], F32) for i in range(2)] for h in range(4)]
            Sb = [[sb(s1, f"Sb{h}_{i}", [128, 128], BF16) for i in range(2)] for h in range(4)]
            ATs = [sb(s1, f"ATs{h}", [128, 128], BF16) for h in range(4)]
            hgo = [[sb(s1, f"hgo{q}_{h}", [128, 128], BF16) for h in range(4)] for q in range(2)]
            ost2 = [sb(s1, f"ost2_{q}", [128, 12], F32) for q in range(2)]
            osb = [sb(s1, f"osb{h}", [128, 128], F32) for h in range(4)]
            hgst = [sb(s1, f"hgst{i}", [128, 4, 512], BF16) for i in range(2)]
            psTr = ps(s1, "psTr", [128, 1024], BF16)
            psF = [ps(s1, f"psF{i}", [128, 512], F32) for i in range(2)]
            psH = [ps(s1, f"psH{h}", [128, 512], F32) for h in range(4)]
            psK = ps(s1, "psK", [128, 1024], BF16)

            for p in range(2):
                for h in range(4):
                    em.op(G, lambda h=h, p=p: nc.gpsimd.memset(QA[p][h][:], 0.0), w=[QA[p][h]])
                    em.op(G, lambda h=h, p=p: nc.gpsimd.memset(QB[p][h][:], 0.0), w=[QB[p][h]])
            for h in range(4):
                em.op(G, lambda h=h: nc.gpsimd.memset(kdA[h][:], 0.0), w=[kdA[h]])
                em.op(G, lambda h=h: nc.gpsimd.memset(kdB[h][:], 0.0), w=[kdB[h]])
                em.op(G, lambda h=h: nc.gpsimd.memset(Sf[h][0][:], 0.0), w=[Sf[h][0]])
                em.op(G, lambda h=h: nc.gpsimd.memset(Sb[h][0][:], 0.0), w=[Sb[h][0]])
            pf_i = [0]

            def next_psF():
                p = psF[pf_i[0] % 2]
                pf_i[0] += 1
                return p

            def load_x(t):
                em.dma(SP, xt[t % 2], xt[t % 2][:], x_d[t * 128:(t + 1) * 128, :], w=[xt[t % 2]])

            def Xa1(t):
                xb_, xnb, st_ = xt[t % 2], xn[t % 2], stt[t % 2]
                em.op(A, lambda: nc.scalar.activation(out=xnb[:], in_=xb_[:], func=AF.Square, accum_out=st_[:, 0:1]), r=[xb_], w=[xnb, st_])
                em.op(A, lambda: nc.scalar.activation(out=st_[:, 1:2], in_=st_[:, 0:1], func=AF.Ln, scale=1.0 / D, bias=cst[:, C_EPS:C_EPS + 1]), r=[st_, cst], w=[st_])
                em.op(A, lambda: nc.scalar.activation(out=st_[:, 2:3], in_=st_[:, 1:2], func=AF.Exp, scale=-0.5), r=[st_], w=[st_])

            def Xa2(t):
                xb_, xnb, st_ = xt[t % 2], xn[t % 2], stt[t % 2]
                em.op(V, lambda: nc.vector.tensor_scalar(out=xnb[:], in0=xb_[:], scalar1=st_[:, 2:3], scalar2=0.0, op0=ALU.mult, op1=ALU.add), r=[xb_, st_], w=[xnb])
                if t + 2 < NT:
                    load_x(t + 2)

            def Xa(t):
                Xa1(t)
                Xa2(t)

            def Xb(t):
                st, j = t // 4, t % 4
                hTc, xnb = hT[st % 2], xn[t % 2]
                for kc in range(8):
                    em.op(P, lambda kc=kc: nc.tensor.transpose(out=psTr[:, kc * 128:(kc + 1) * 128], in_=xnb[:, kc * 128:(kc + 1) * 128], identity=identb[:]),
                          r=[xnb, identb], w=[psTr], signal=(kc == 7))
                for kc in range(8):
                    o = hTc[:, kc, j * 128:(j + 1) * 128]
                    i_ = psTr[:, kc * 128:(kc + 1) * 128]
                    if t % 2 == 0:
                        em.op(V, lambda o=o, i_=i_, kc=kc: nc.vector.tensor_scalar(out=o, in0=i_, scalar1=cst[:, C_G1 + kc:C_G1 + kc + 1], scalar2=cst[:, C_S1 + kc:C_S1 + kc + 1], op0=ALU.mult, op1=ALU.add),
                              r=[psTr, cst], w=[hTc])
                    else:
                        em.op(A, lambda o=o, i_=i_, kc=kc: nc.scalar.activation(out=o, in_=i_, func=AF.Identity, scale=cst[:, C_G1 + kc:C_G1 + kc + 1], bias=cst[:, C_S1 + kc:C_S1 + kc + 1]),
                              r=[psTr, cst], w=[hTc])

            def proj_fm(hTc, c0):
                pf = next_psF()
                for kc in range(8):
                    em.op(P, lambda kc=kc: nc.tensor.matmul(pf[:, :], lhsT=win_bf[:, kc, c0:c0 + 128], rhs=hTc[:, kc, :], start=(kc == 0), stop=(kc == 7)),
                          r=[hTc, win_bf], w=[pf], signal=(kc == 7))
                return pf

            def EW(st, h, f):
                p = st % 2
                em.op(A, lambda: nc.scalar.activation(out=lf[:], in_=sgm[f][:], func=AF.Ln, scale=cst[:, C_OML + h:C_OML + h + 1], bias=cst[:, C_LB + h:C_LB + h + 1]), r=[sgm[f], cst], w=[lf])
                em.op(V, lambda: nc.vector.tensor_tensor_scan(out=bcs[:], data0=mscan[:], data1=lf[:], initial=0.0, op0=ALU.mult, op1=ALU.add), r=[mscan, lf], w=[bcs])
                em.op(A, lambda: nc.scalar.activation(out=eb[f][:], in_=bcs[:], func=AF.Exp), r=[bcs], w=[eb[f]])
                em.op(A, lambda: nc.scalar.activation(out=enb[:], in_=bcs[:], func=AF.Exp, scale=-1.0), r=[bcs], w=[enb])
                qv = qs[f][:].rearrange("p (t e c) -> p t e c", e=2, c=64)
                ev = eb[f][:].rearrange("p (t e c) -> p t e c", e=2, c=64)
                qav = QA[p][h][:].rearrange("p (t e c) -> p t e c", e=2, c=64)
                qbv = QB[p][h][:].rearrange("p (t e c) -> p t e c", e=2, c=64)
                em.op(G, lambda: nc.gpsimd.tensor_tensor(out=qav[:, :, 0, :], in0=qv[:, :, 0, :], in1=ev[:, :, 0, :], op=ALU.mult), r=[qs[f], eb[f]], w=[QA[p][h]])
                em.op(G, lambda: nc.gpsimd.tensor_tensor(out=qbv[:, :, 1, :], in0=qv[:, :, 1, :], in1=ev[:, :, 1, :], op=ALU.mult), r=[qs[f], eb[f]], w=[QB[p][h]])
                em.op(V, lambda: nc.vector.tensor_scalar(out=kk[:], in0=sgm[f][:], scalar1=cst[:, C_NOML + h:C_NOML + h + 1], scalar2=cst[:, C_OML + h:C_OML + h + 1], op0=ALU.mult, op1=ALU.add),
                      r=[sgm[f], cst], w=[kk])
                em.op(V, lambda: nc.vector.tensor_tensor(out=tmpk[:], in0=kk[:], in1=enb[:], op=ALU.mult), r=[kk, enb], w=[tmpk])
                em.op(G, lambda: nc.gpsimd.tensor_copy(out=kt[p][h][:], in_=tmpk[:]), r=[tmpk], w=[kt[p][h]])
                ebv = eb[f][:].rearrange("p (c t) -> p c t", t=64)
                em.op(V, lambda: nc.vector.tensor_tensor(out=kd[p][h][:].rearrange("p (c t) -> p c t", t=64), in0=tmpk[:].rearrange("p (c t) -> p c t", t=64),
                                                         in1=ebv[:, :, 63:64].to_broadcast([128, 8, 64]), op=ALU.mult), r=[tmpk, eb[f]], w=[kd[p][h]])
                em.op(G, lambda: nc.gpsimd.tensor_copy(out=ebl[p][h][:].rearrange("p (c o) -> p c o", o=1), in_=ebv[:, :, 63:64]), r=[eb[f]], w=[ebl[p][h]])

            def IP(st):
                hTc = hT[st % 2]
                vt, sg_t = vtok[st % 2], sgg[st % 2]
                for h in range(4):
                    f = (st * 4 + h) % NF
                    pq = proj_fm(hTc, h * 128)
                    em.op(A, lambda: nc.scalar.activation(out=qs[f][:], in_=pq[:, :], func=AF.Silu), r=[pq], w=[qs[f]])
                    pfg = proj_fm(hTc, 512 + h * 128)
                    em.op(A, lambda: nc.scalar.activation(out=sgm[f][:], in_=pfg[:, :], func=AF.Sigmoid), r=[pfg], w=[sgm[f]])
                    EW(st, h, f)
                for j in range(4):
                    for gi, c0 in enumerate((1024, 1536, 3072)):
                        pf = next_psF()
                        for kc in range(8):
                            em.op(P, lambda kc=kc, pf=pf, c0=c0, j=j: nc.tensor.matmul(pf[:, :], lhsT=hTc[:, kc, j * 128:(j + 1) * 128], rhs=win_bf[:, kc, c0:c0 + 512], start=(kc == 0), stop=(kc == 7)),
                                  r=[hTc, win_bf], w=[pf], signal=(kc == 7))
                        if gi == 0:
                            em.op(V, lambda pf=pf, j=j: nc.vector.tensor_copy(out=vt[:, j, :], in_=pf[:, :]), r=[pf], w=[vt])
                        elif gi == 1:
                            em.op(A, lambda pf=pf, j=j: nc.scalar.activation(out=sg_t[:, j, :], in_=pf[:, :], func=AF.Silu), r=[pf], w=[sg_t])
                        else:
                            em.op(V, lambda pf=pf, j=j: nc.vector.tensor_copy(out=avst[:, j, :], in_=pf[:, :]), r=[pf], w=[avst])
                em.dma(SP, avst, v_d[st * 512:(st + 1) * 512, :].rearrange("(j p) f -> p j f", p=128), avst[:], r=[avst], w=[v_B])
                for pr in range(4):
                    pf = proj_fm(hTc, 2048 + pr * 128)
                    em.op(A, lambda pf=pf, pr=pr: nc.scalar.mul(out=qst[:, pr, :], in_=pf[:, :], mul=0.125), r=[pf], w=[qst])
                    pf = proj_fm(hTc, 2560 + pr * 128)
                    em.op(V, lambda pf=pf, pr=pr: nc.vector.tensor_copy(out=kst[:, pr, :], in_=pf[:, :]), r=[pf], w=[kst])
                em.dma(SP, qst, qT_d.rearrange("(pr two) e t -> (two e) pr t", two=2)[:, :, st * 512:(st + 1) * 512], qst[:], r=[qst], w=[qT_B])
                em.dma(SP, kst, kT_d.rearrange("(pr two) e t -> (two e) pr t", two=2)[:, :, st * 512:(st + 1) * 512], kst[:], r=[kst], w=[kT_B])

            def REC(st, hooks):
                p = st % 2
                vt, sg_t, hg_t = vtok[p], sgg[p], hgst[p]
                for h in range(4):
                    for j in range(4):
                        em.op(P, lambda j=j, h=h: nc.tensor.transpose(out=psK[:, j * 128:(j + 1) * 128], in_=kd[p][h][:, j * 128:(j + 1) * 128], identity=identb[:]),
                              r=[kd[p][h], identb], w=[psK], signal=(j == 3))
                    em.op(V, lambda h=h: nc.vector.tensor_copy(out=kdA[h][0:64, :, :].rearrange("p j k -> p (j k)"), in_=psK[0:64, 0:512]), r=[psK], w=[kdA[h]])
                    em.op(V, lambda h=h: nc.vector.tensor_copy(out=kdB[h][64:128, :, :].rearrange("p j k -> p (j k)"), in_=psK[64:128, 0:512]), r=[psK], w=[kdB[h]])
                def emit_T(jj):
                    for h in range(4):
                        em.op(P, lambda h=h: nc.tensor.transpose(out=psK[:, 512 + h * 128:512 + (h + 1) * 128], in_=hgo[jj % 2][h][:], identity=identb[:]), r=[hgo[jj % 2][h], identb], w=[psK], signal=(h == 3))
                    em.op(V, lambda: nc.vector.tensor_copy(out=hg_t[:, :, jj * 128:(jj + 1) * 128], in_=psK[:, 512:1024].rearrange("p (h t) -> p h t", t=128)), r=[psK], w=[hg_t])

                for j in range(4):
                    if hooks[j][0]:
                        hooks[j][0]()
                    cs = slice(j * 128, (j + 1) * 128)
                    for h in range(4):
                        hc = slice(h * 128, (h + 1) * 128)
                        pH = psH[h]
                        em.op(P, lambda pH=pH, h=h: nc.tensor.matmul(pH[:, 0:128], lhsT=kt[p][h][:, cs], rhs=QA[p][h][:, cs], start=True, stop=False), r=[kt[p][h], QA[p][h]], w=[pH], signal=False)
                        em.op(P, lambda pH=pH, h=h: nc.tensor.matmul(pH[:, 0:128], lhsT=kt[p][h][:, cs], rhs=QB[p][h][:, cs], start=False, stop=True), r=[kt[p][h], QB[p][h]], w=[pH], signal=False)
                        em.op(P, lambda pH=pH, h=h, hc=hc: nc.tensor.matmul(pH[:, 256:384], lhsT=kdA[h][:, j, :], rhs=vt[:, j, hc], start=True, stop=True), r=[kdA[h], vt], w=[pH], signal=False)
                        em.op(P, lambda pH=pH, h=h, hc=hc: nc.tensor.matmul(pH[:, 384:512], lhsT=kdB[h][:, j, :], rhs=vt[:, j, hc], start=True, stop=True), r=[kdB[h], vt], w=[pH], signal=True)
                        em.op(V, lambda pH=pH, h=h: nc.vector.tensor_tensor(out=ATs[h][:], in0=pH[:, 0:128], in1=mhg[:], op=ALU.mult), r=[pH, mhg], w=[ATs[h]])
                    for h in range(4):
                        pH = psH[h]
                        em.op(V, lambda pH=pH, h=h: nc.vector.scalar_tensor_tensor(out=Sf[h][1][:], in0=Sf[h][0][:], scalar=ebl[p][h][:, 2 * j:2 * j + 1], in1=pH[:, 256:384], op0=ALU.mult, op1=ALU.add),
                              r=[Sf[h][0], ebl[p][h], pH], w=[Sf[h][1]])
                        if h % 2 == 0:
                            em.op(G, lambda h=h: nc.gpsimd.tensor_copy(out=Sb[h][1][:], in_=Sf[h][1][:]), r=[Sf[h][1]], w=[Sb[h][1]])
                        else:
                            em.op(A, lambda h=h: nc.scalar.copy(out=Sb[h][1][:], in_=Sf[h][1][:]), r=[Sf[h][1]], w=[Sb[h][1]])
                    if hooks[j][1]:
                        hooks[j][1]()
                    for h in range(4):
                        hc = slice(h * 128, (h + 1) * 128)
                        pH = psH[h]
                        em.op(P, lambda pH=pH, h=h: nc.tensor.matmul(pH[:, 128:256], lhsT=QA[p][h][:, cs], rhs=Sb[h][0][:], start=True, stop=False), r=[QA[p][h], Sb[h][0]], w=[pH], signal=False)
                        em.op(P, lambda pH=pH, h=h: nc.tensor.matmul(pH[:, 128:256], lhsT=QB[p][h][:, cs], rhs=Sb[h][1][:], start=False, stop=False), r=[QB[p][h], Sb[h][1]], w=[pH], signal=False)
                        em.op(P, lambda pH=pH, h=h, hc=hc: nc.tensor.matmul(pH[:, 128:256], lhsT=ATs[h][:], rhs=vt[:, j, hc], start=False, stop=True), r=[ATs[h], vt], w=[pH], signal=True)
                    if j > 0:
                        emit_T(j - 1)
                    o2 = ost2[j % 2]
                    for h in range(4):
                        pH = psH[h]
                        hg_ = hgo[j % 2][h]
                        em.op(V, lambda pH=pH, h=h: nc.vector.tensor_copy(out=osb[h][:], in_=pH[:, 128:256]), r=[pH], w=[osb[h]])
                        em.op(V, lambda pH=pH, h=h: nc.vector.scalar_tensor_tensor(out=Sf[h][0][:], in0=Sf[h][1][:], scalar=ebl[p][h][:, 2 * j + 1:2 * j + 2], in1=pH[:, 384:512], op0=ALU.mult, op1=ALU.add),
                              r=[Sf[h][1], ebl[p][h], pH], w=[Sf[h][0]])
                        em.op(G, lambda h=h: nc.gpsimd.tensor_copy(out=Sb[h][0][:], in_=Sf[h][0][:]), r=[Sf[h][0]], w=[Sb[h][0]])
                        em.op(A, lambda o2=o2, h=h, hg_=hg_: nc.scalar.activation(out=hg_[:], in_=osb[h][:], func=AF.Square, accum_out=o2[:, h:h + 1]), r=[osb[h]], w=[hg_, o2])
                    em.op(A, lambda o2=o2: nc.scalar.activation(out=o2[:, 4:8], in_=o2[:, 0:4], func=AF.Ln, scale=1.0 / 128, bias=cst[:, C_EPS:C_EPS + 1]), r=[o2, cst], w=[o2])
                    em.op(A, lambda o2=o2: nc.scalar.activation(out=o2[:, 8:12], in_=o2[:, 4:8], func=AF.Exp, scale=-0.5), r=[o2], w=[o2])
                    for h in range(4):
                        hc = slice(h * 128, (h + 1) * 128)
                        hg_ = hgo[j % 2][h]
                        em.op(V, lambda o2=o2, hg_=hg_, hc=hc, h=h: nc.vector.scalar_tensor_tensor(out=hg_[:], in0=osb[h][:], scalar=o2[:, 8 + h:9 + h], in1=sg_t[:, j, hc], op0=ALU.mult, op1=ALU.mult),
                              r=[osb[h], o2, sg_t], w=[hg_])
                    if hooks[j][2]:
                        hooks[j][2]()
                emit_T(3)
                for h in range(4):
                    em.dma(SP, hg_t, mix_d[st * 4:(st + 1) * 4, :, h, :].rearrange("j p t -> p j t"), hg_t[:, h, :].rearrange("p (j t) -> p j t", t=128), r=[hg_t], w=[mix_B])

            load_x(0)
            load_x(1)
            for t in range(4):
                Xa(t)
                Xb(t)
            IP(0)
            for t in range(4, 8):
                Xa(t)
                Xb(t)
            Xa(8)
            for st in range(NST):
                if st + 1 < NST:
                    IP(st + 1)
                hooks = []
                for j in range(4):
                    t = (st + 2) * 4 + j
                    if t < NT:
                        h0 = (lambda t=t: Xa1(t + 1)) if t + 1 < NT else None
                        h1 = (lambda t=t: Xb(t))
                        h2 = (lambda t=t: Xa2(t + 1)) if t + 1 < NT else None
                        hooks.append((h0, h1, h2))
                    else:
                        hooks.append((None, None, None))
                REC(st, hooks)
        em.barrier()

        wout_bf = sb(es, "wout_bf", [128, 8, D], BF16)
        wdnA = sb(es, "wdnA", [128, 11, D], BF16)

        with ExitStack() as s2:
            stage2 = [sb(s2, f"wstg2_{i}", [128, D], F32) for i in range(2)]
            wtasks = cast_tasks(wout_bf, wout_d, 8, D, stage2,
                                (lambda kc: (Vc[:, R_HG:R_HG + 1] if kc < 4 else Vc[:, R_AG + kc - 4:R_AG + kc - 3]), gate_bc[:, 0:D]))
            wtasks += cast_tasks(wdnA, wdn_d[0:11 * 128, :], 11, D, stage2, (None, gate_bc[:, D:2 * D]))
            QT = [sb(s2, f"QT{i}", [128, S], BF16) for i in range(2)]
            KT = [sb(s2, f"KT{i}", [128, S], BF16) for i in range(2)]
            for i in range(2):
                em.op(G, lambda i=i: nc.gpsimd.memset(QT[i][:], 0.0), w=[QT[i]])
            Vh = [[sb(s2, f"Vh{i}_{d}", [128, 32, 65], BF16) for d in range(3)] for i in range(2)]
            acc = sb(s2, "acc", [65, S], F32)
            attT = sb(s2, "attT", [128, 4, S], BF16)
            ssb = sb(s2, "ssb", [128, S], F32)
            PT = [sb(s2, f"PT{i}", [128, 512], BF16) for i in range(3)]
            rl = [sb(s2, f"rl{i}", [64, 512], F32) for i in range(2)]
            attf = [sb(s2, f"attf{i}", [64, 512], F32) for i in range(2)]
            sq = [sb(s2, f"sq{i}", [64, 512], BF16) for i in range(4)]
            rsd = [sb(s2, f"rsd{i}", [128, 512], F32) for i in range(2)]
            psS = [ps(s2, f"psS{i}", [128, 512], F32) for i in range(3)]
            psO = [ps(s2, f"psO{i}", [128, 512], F32) for i in range(4)]
            psL = ps(s2, "psL", [128, 512], F32)
            for i in range(2):
                for d in range(3):
                    em.op(G, lambda i=i, d=d: nc.gpsimd.memset(Vh[i][d][:, :, 64:65], 1.0), w=[Vh[i][d]])
            DIL = (1, 4, 16)

            def load_head(h):
                i = h % 2
                em.dma(SP, QT[i], QT[i][64 * i:64 * i + 64, :], qT_d[h], r=[qT_B], w=[QT[i]])
                if h % 2 == 0:
                    kp = (h // 2) % 2
                    em.dma(SP, KT[kp], KT[kp][:], kT_d[h:h + 2].rearrange("two e t -> (two e) t"), r=[kT_B], w=[KT[kp]])
                for di, d in enumerate(DIL):
                    nb = 32 // d
                    src = v_d[:, h * 64:(h + 1) * 64].rearrange("(n i r) e -> i r n e", i=128, r=d)
                    if d <= nb:
                        for r_ in range(d):
                            em.dma(SP, Vh[i][di], Vh[i][di][:, r_ * nb:(r_ + 1) * nb, 0:64], src[:, r_, :, :], r=[v_B], w=[Vh[i][di]])
                    else:
                        for n_ in range(nb):
                            em.dma(SP, Vh[i][di], Vh[i][di][:, n_:32:nb, 0:64], src[:, :, n_, :], r=[v_B], w=[Vh[i][di]])

            load_head(0)
            s_i = [0]
            deferred = []
            LAG = 2
            for h in range(8):
                i = h % 2
                if h + 1 < 8:
                    load_head(h + 1)
                pr, half = h // 2, h % 2
                Q, K = QT[i], KT[pr % 2]
                for R in range(2):
                    blocks = []
                    for n in range(16 * R, 16 * R + 16):
                        blocks.append((0, 1, 0, n))
                    for n in range(4 * R, 4 * R + 4):
                        for r_ in range(4):
                            blocks.append((1, 4, r_, n))
                    for r_ in range(16):
                        blocks.append((2, 16, r_, R))
                    pairs = [blocks[k:k + 2] for k in range(0, len(blocks), 2)]
                    touched = set()
                    pv_ops = []
                    for pi, pair in enumerate(pairs):
                        ops = []
                        for bi, (di, d, r_, n) in enumerate(pair):
                            nb = 32 // d
                            blk = r_ * nb + n
                            c0 = bi * 256
                            for half_, vb in ((0, blk - 1), (1, blk)):
                                if half_ == 0 and n == 0:
                                    continue
                                pc = c0 + half_ * 128
                                if d == 1:
                                    b_ = (n % 16) // 4
                                    ops.append((b_, slice((n % 4) * 128, (n % 4) * 128 + 128), di, vb, slice(pc, pc + 128)))
                                elif d == 4:
                                    b_ = n % 4
                                    ops.append((b_, slice(r_, r_ + 4 * 127 + 1, 4), di, vb, slice(pc, pc + 128)))
                                else:
                                    for b_ in range(4):
                                        ops.append((b_, slice(r_, r_ + 16 * 31 + 1, 16), di, vb, slice(pc + 32 * b_, pc + 32 * b_ + 32)))
                        pv_ops.append(ops)
                    last_touch = {}
                    for pi, ops in enumerate(pv_ops):
                        for oi, o_ in enumerate(ops):
                            last_touch[o_[0]] = (pi, oi)
                    stage_buf = {}

                    def emit_qk(pi, pairs=pairs, Q=Q, K=K, stage_buf=stage_buf):
                        pS = psS[s_i[0] % 3]
                        pt = PT[s_i[0] % 3]
                        s_i[0] += 1
                        stage_buf[pi] = pt
                        for bi, (di, d, r_, n) in enumerate(pairs[pi]):
                            q0 = n * 128 * d + r_
                            qsl = slice(q0, q0 + 127 * d + 1, d)
                            c0 = bi * 256
                            p0 = (n - 1) * 128 * d + r_ if n > 0 else q0
                            psl = slice(p0, p0 + 127 * d + 1, d)
                            em.op(P, lambda pS=pS, c0=c0, psl=psl, qsl=qsl: nc.tensor.matmul(pS[:, c0:c0 + 128], lhsT=K[:, psl], rhs=Q[:, qsl], start=True, stop=True), r=[K, Q], w=[pS], signal=False)
                            em.op(P, lambda pS=pS, c0=c0, qsl=qsl: nc.tensor.matmul(pS[:, c0 + 128:c0 + 256], lhsT=K[:, qsl], rhs=Q[:, qsl], start=True, stop=True), r=[K, Q], w=[pS], signal=(bi == 1))
                        em.op(A, lambda pS=pS, pt=pt: nc.scalar.activation(out=pt[:], in_=pS[:, :], func=AF.Exp), r=[pS], w=[pt])
                        em.op(V, lambda pt=pt: nc.vector.tensor_tensor(out=pt[:], in0=pt[:], in1=matt[:], op=ALU.mult), r=[pt, matt], w=[pt])

                    def emit_pv(pi, pv_ops=pv_ops, stage_buf=stage_buf, touched=touched, last_touch=last_touch, i=i):
                        pt = stage_buf[pi]
                        ops = pv_ops[pi]
                        for oi, (b_, osl, di, vb, psl) in enumerate(ops):
                            first = b_ not in touched
                            touched.add(b_)
                            last = last_touch[b_] == (pi, oi)
                            Vd = Vh[i][di]
                            em.op(P, lambda b_=b_, osl=osl, Vd=Vd, vb=vb, psl=psl, pt=pt, first=first, last=last: nc.tensor.matmul(psO[b_][0:65, osl], lhsT=Vd[:, vb, :], rhs=pt[:, psl], start=first, stop=last, skip_group_check=True),
                                  r=[Vd, pt], w=[psO[b_]], signal=(oi == len(ops) - 1))

                    npairs = len(pairs)
                    for idx in range(npairs + LAG):
                        if idx < npairs:
                            emit_qk(idx)
                        if idx >= LAG:
                            emit_pv(idx - LAG)
                        if deferred and idx % 2 == 1:
                            deferred.pop(0)()
                        if wtasks and idx % 8 == 4:
                            wtasks.pop(0)()
                    while deferred:
                        deferred.pop(0)()
                    for b_ in range(4):
                        csl = slice(R * 2048 + b_ * 512, R * 2048 + (b_ + 1) * 512)
                        if b_ % 2 == 0:
                            em.op(V, lambda b_=b_, csl=csl: nc.vector.tensor_copy(out=acc[0:65, csl], in_=psO[b_][0:65, :]), r=[psO[b_]], w=[acc])
                        else:
                            em.op(A, lambda b_=b_, csl=csl: nc.scalar.copy(out=acc[0:65, csl], in_=psO[b_][0:65, :]), r=[psO[b_]], w=[acc])
                    fin1s, fin2s = [], []
                    for b_ in range(4):
                        c = R * 4 + b_
                        csl = slice(c * 512, (c + 1) * 512)

                        def fin1(c=c, csl=csl, pr=pr, half=half, h=h):
                            rl_, af_, sq_ = rl[c % 2], attf[c % 2], sq[c % 4]
                            em.op(P, lambda: nc.tensor.matmul(psL[0:64, :], lhsT=onesf[64:65, 0:64], rhs=acc[64:65, csl], start=True, stop=True), r=[onesf, acc], w=[psL], signal=True)
                            em.op(A, lambda: nc.scalar.activation(out=rl_[:], in_=psL[0:64, :], func=AF.Ln), r=[psL], w=[rl_])
                            em.op(A, lambda: nc.scalar.activation(out=rl_[:], in_=rl_[:], func=AF.Exp, scale=-1.0), r=[rl_], w=[rl_])
                            em.op(V, lambda: nc.vector.tensor_tensor(out=af_[:], in0=acc[0:64, csl], in1=rl_[:], op=ALU.mult), r=[acc, rl_], w=[af_])
                            em.op(A, lambda: nc.scalar.copy(out=attT[64 * half:64 * half + 64, pr, csl], in_=af_[:]), r=[af_], w=[attT])
                            em.op(G, lambda: nc.gpsimd.tensor_tensor(out=sq_[:], in0=af_[:], in1=af_[:], op=ALU.mult), r=[af_], w=[sq_])

                        def fin2(c=c, csl=csl, h=h):
                            sq_ = sq[c % 4]
                            em.op(P, lambda: nc.tensor.matmul(psL[:, :], lhsT=onesb[0:64, :], rhs=sq_[:], start=True, stop=True), r=[onesb, sq_], w=[psL], signal=True)
                            if h == 0:
                                em.op(V, lambda: nc.vector.tensor_copy(out=ssb[:, csl], in_=psL[:, :]), r=[psL], w=[ssb])
                            else:
                                em.op(V, lambda: nc.vector.tensor_tensor(out=ssb[:, csl], in0=ssb[:, csl], in1=psL[:, :], op=ALU.add), r=[psL, ssb], w=[ssb])
                        fin1s.append(fin1)
                        fin2s.append(fin2)
                    deferred.extend(fin1s)
                    deferred.extend(fin2s)
            while deferred:
                deferred.pop(0)()
            while wtasks:
                wtasks.pop(0)()
            for c in range(8):
                csl = slice(c * 512, (c + 1) * 512)
                r_ = rsd[c % 2]
                em.op(A, lambda r_=r_, csl=csl: nc.scalar.activation(out=r_[:], in_=ssb[:, csl], func=AF.Ln, scale=1.0 / 512, bias=cst[:, C_EPS:C_EPS + 1]), r=[ssb, cst], w=[r_])
                em.op(A, lambda r_=r_: nc.scalar.activation(out=r_[:], in_=r_[:], func=AF.Exp, scale=-0.5), r=[r_], w=[r_])
                for pr in range(4):
                    e = V
                    eng = nc.vector
                    em.op(e, lambda r_=r_, csl=csl, pr=pr, eng=eng: eng.tensor_tensor(out=attT[:, pr, csl], in0=attT[:, pr, csl], in1=r_[:], op=ALU.mult), r=[attT, r_], w=[attT])
            for pr in range(4):
                em.dma(SP, attT, mix_d[:, :, 4 + pr, :].rearrange("j p t -> p j t"), attT[:, pr, :].rearrange("p (j t) -> p j t", t=128), r=[attT], w=[mix_B])
        em.barrier()

        with ExitStack() as s3:
            fg_bc = sb(s3, "fg_bc", [128, D], F32)
            em.dma(SP, fg_bc, fg_bc[:], fg_d.partition_broadcast(128), w=[fg_bc])
            wgu_bf = sb(s3, "wgu_bf", [128, 8, 2 * DFF], BF16)
            wdnB = sb(s3, "wdnB", [128, 11, D], BF16)
            with ExitStack() as s3a:
                stage = [sb(s3a, f"wstg3_{i}", [128, 2048], F32) for i in range(3)]
                cast_weight(wgu_bf, wgu_d, 8, 2 * DFF, stage)
                cast_weight(wdnB, wdn_d[11 * 128:22 * 128, :], 11, D, stage, fold=(None, gate_bc[:, D:2 * D]))
            em.barrier()

            xt3 = [sb(s3, f"xt3_{i}", [128, D], F32) for i in range(3)]
            mxt = [sb(s3, f"mxt{i}", [128, 8, 128], BF16) for i in range(2)]
            junk3 = sb(s3, "junk3", [128, D], BF16)
            xn2 = [sb(s3, "xn2_0", [128, D], BF16)] * 2
            h2T = [sb(s3, f"h2T{i}", [128, 8, 128], BF16) for i in range(2)]
            st3 = [sb(s3, f"st3_{i}", [128, 8], F32) for i in range(2)]
            sil = [sb(s3, f"sil{i}", [128, 512], F32) for i in range(2)]
            actb = [sb(s3, f"actb{i}", [128, 512], BF16) for i in range(2)]
            actT = [[sb(s3, f"actT{i}_{g}", [128, 4 if g < 5 else 2, 128], BF16) for g in range(6)] for i in range(2)]
            psX = ps(s3, "psX", [128, 1024], F32)
            psA = [ps(s3, f"psA{i}", [128, 512], F32) for i in range(2)]
            psU = [ps(s3, f"psU{i}", [128, 512], F32) for i in range(2)]
            psT3 = [ps(s3, f"psT3_{i}", [128, 1024], BF16) for i in range(2)]

            def load3(t):
                em.dma(SP, mxt[t % 2], mxt[t % 2][:], mix_d[t], r=[mix_B], w=[mxt[t % 2]])


            def loadx3(t):
                em.dma(SP, xt3[t % 3], xt3[t % 3][:], x_d[t * 128:(t + 1) * 128, :], r=[out_B] if False else [], w=[xt3[t % 3]])

            load3(0); loadx3(0); load3(1); loadx3(1); loadx3(2)
            tr_i = [0]

            def S1(t):
                xb_, mx = xt3[t % 3], mxt[t % 2]
                for nh in range(2):
                    for kc in range(8):
                        em.op(P, lambda nh=nh, kc=kc, mx=mx: nc.tensor.matmul(psX[:, nh * 512:(nh + 1) * 512], lhsT=mx[:, kc, :], rhs=wout_bf[:, kc, nh * 512:(nh + 1) * 512], start=(kc == 0), stop=(kc == 7)),
                              r=[mx, wout_bf], w=[psX], signal=(kc == 7 and nh == 1))
                em.op(V, lambda xb_=xb_: nc.vector.tensor_tensor(out=xb_[:], in0=xb_[:], in1=psX[:, :], op=ALU.add), r=[xb_, psX], w=[xb_])
                if t + 2 < NT:
                    load3(t + 2)
                s_ = st3[t % 2]
                em.op(A, lambda xb_=xb_, s_=s_: nc.scalar.activation(out=junk3[:], in_=xb_[:], func=AF.Square, accum_out=s_[:, 0:1]), r=[xb_], w=[junk3, s_])
                em.op(A, lambda s_=s_: nc.scalar.activation(out=s_[:, 1:2], in_=s_[:, 0:1], func=AF.Ln, scale=1.0 / D, bias=cst[:, C_EPS:C_EPS + 1]), r=[s_, cst], w=[s_])
                em.op(A, lambda s_=s_: nc.scalar.activation(out=s_[:, 2:3], in_=s_[:, 1:2], func=AF.Exp, scale=-0.5), r=[s_], w=[s_])
                xnb = xn2[t % 2]
                em.op(A, lambda xb_=xb_, xnb=xnb, s_=s_: nc.scalar.activation(out=xnb[:], in_=xb_[:], func=AF.Identity, scale=s_[:, 2:3]), r=[xb_, s_], w=[xnb])

            def S2(t):
                xnb = xn2[t % 2]
                pT = psT3[tr_i[0] % 2]; tr_i[0] += 1
                for kc in range(8):
                    em.op(P, lambda kc=kc, xnb=xnb, pT=pT: nc.tensor.transpose(out=pT[:, kc * 128:(kc + 1) * 128], in_=xnb[:, kc * 128:(kc + 1) * 128], identity=identb[:]), r=[xnb, identb], w=[pT], signal=(kc == 7))
                hb = h2T[t % 2]
                for kc in range(8):
                    o = hb[:, kc, :]
                    i_ = pT[:, kc * 128:(kc + 1) * 128]
                    if t % 2 == 0:
                        em.op(V, lambda o=o, i_=i_, kc=kc: nc.vector.tensor_scalar(out=o, in0=i_, scalar1=cst[:, C_G2 + kc:C_G2 + kc + 1], scalar2=cst[:, C_S2 + kc:C_S2 + kc + 1], op0=ALU.mult, op1=ALU.add), r=[pT, cst], w=[hb])
                    else:
                        em.op(A, lambda o=o, i_=i_, kc=kc: nc.scalar.activation(out=o, in_=i_, func=AF.Identity, scale=cst[:, C_G2 + kc:C_G2 + kc + 1], bias=cst[:, C_S2 + kc:C_S2 + kc + 1]), r=[pT, cst], w=[hb])

            def S3g(t, g):
                hb = h2T[t % 2]
                aT = actT[t % 2]
                c0 = g * 512
                w_ = min(512, DFF - c0)
                pa, pu = psA[g % 2], psU[g % 2]
                for kc in range(8):
                    em.op(P, lambda kc=kc, pa=pa, c0=c0, w_=w_, hb=hb: nc.tensor.matmul(pa[:, 0:w_], lhsT=hb[:, kc, :], rhs=wgu_bf[:, kc, c0:c0 + w_], start=(kc == 0), stop=(kc == 7)), r=[hb, wgu_bf], w=[pa], signal=(kc == 7))
                for kc in range(8):
                    em.op(P, lambda kc=kc, pu=pu, c0=c0, w_=w_, hb=hb: nc.tensor.matmul(pu[:, 0:w_], lhsT=hb[:, kc, :], rhs=wgu_bf[:, kc, DFF + c0:DFF + c0 + w_], start=(kc == 0), stop=(kc == 7)), r=[hb, wgu_bf], w=[pu], signal=(kc == 7))
                sl_, ab_ = sil[g % 2], actb[g % 2]
                em.op(A, lambda pa=pa, sl_=sl_, w_=w_: nc.scalar.activation(out=sl_[:, 0:w_], in_=pa[:, 0:w_], func=AF.Silu), r=[pa], w=[sl_])
                em.op(V, lambda pu=pu, sl_=sl_, ab_=ab_, w_=w_: nc.vector.tensor_tensor(out=ab_[:, 0:w_], in0=sl_[:, 0:w_], in1=pu[:, 0:w_], op=ALU.mult), r=[sl_, pu], w=[ab_])

            def S3t(t, g):
                aT = actT[t % 2]
                c0 = g * 512
                w_ = min(512, DFF - c0)
                ab_ = actb[g % 2]
                pT = psT3[tr_i[0] % 2]; tr_i[0] += 1
                nck = w_ // 128
                for q in range(nck):
                    em.op(P, lambda q=q, ab_=ab_, pT=pT: nc.tensor.transpose(out=pT[:, q * 128:(q + 1) * 128], in_=ab_[:, q * 128:(q + 1) * 128], identity=identb[:]), r=[ab_, identb], w=[pT], signal=(q == nck - 1))
                if g % 2 == 0:
                    em.op(A, lambda pT=pT, aT=aT, g=g, nck=nck: nc.scalar.copy(out=aT[g][:, :, :].rearrange("p a b -> p (a b)"), in_=pT[:, 0:nck * 128]), r=[pT], w=[aT[g]])
                else:
                    em.op(V, lambda pT=pT, aT=aT, g=g, nck=nck: nc.vector.tensor_copy(out=aT[g][:, :, :].rearrange("p a b -> p (a b)"), in_=pT[:, 0:nck * 128]), r=[pT], w=[aT[g]])

            def S4(t):
                xb_ = xt3[t % 3]
                aT = actT[t % 2]
                s_ = st3[t % 2]
                for nh in range(2):
                    for kc in range(22):
                        em.op(P, lambda nh=nh, kc=kc, aT=aT: nc.tensor.matmul(psX[:, nh * 512:(nh + 1) * 512], lhsT=aT[kc // 4][:, kc % 4, :], rhs=(wdnA if kc < 11 else wdnB)[:, kc % 11, nh * 512:(nh + 1) * 512], start=(kc == 0), stop=(kc == 21)),
                              r=[aT[kc // 4], wdnA, wdnB], w=[psX], signal=(kc == 21 and nh == 1))
                em.op(V, lambda xb_=xb_: nc.vector.tensor_tensor(out=xb_[:], in0=xb_[:], in1=psX[:, :], op=ALU.add), r=[xb_, psX], w=[xb_])
                em.op(A, lambda xb_=xb_, s_=s_: nc.scalar.activation(out=junk3[:], in_=xb_[:], func=AF.Square, accum_out=s_[:, 4:5]), r=[xb_], w=[junk3, s_])
                em.op(A, lambda s_=s_: nc.scalar.activation(out=s_[:, 5:6], in_=s_[:, 4:5], func=AF.Ln, scale=1.0 / D, bias=cst[:, C_EPS:C_EPS + 1]), r=[s_, cst], w=[s_])
                em.op(A, lambda s_=s_: nc.scalar.activation(out=s_[:, 6:7], in_=s_[:, 5:6], func=AF.Exp, scale=-0.5), r=[s_], w=[s_])
                em.op(V, lambda xb_=xb_, s_=s_: nc.vector.scalar_tensor_tensor(out=xb_[:], in0=xb_[:], scalar=s_[:, 6:7], in1=fg_bc[:], op0=ALU.mult, op1=ALU.mult), r=[xb_, s_, fg_bc], w=[xb_])
                em.dma(SP, xb_, out_d[t * 128:(t + 1) * 128, :], xb_[:], r=[xb_])

            S1(0)
            S2(0)
            for t in range(NT):
                if t >= 1 and t + 2 < NT:
                    loadx3(t + 2)
                S3g(t, 0)
                if t + 1 < NT:
                    S1(t + 1)
                for g in range(1, 6):
                    S3g(t, g)
                    S3t(t, g - 1)
                if t + 1 < NT:
                    S2(t + 1)
                S3t(t, 5)
                S4(t)
        em.finish()
        print(f"[build] instructions={em.nins} waits={em.nwaits}")
    return nc


def _consts():
    ident = np.eye(128, dtype=np.float32)
    s = np.arange(128)[:, None]
    t = np.arange(128)[None, :]
    mask_hg = ((s <= t) & ((s // 64) == (t // 64))).astype(np.float32)
    prev = (s >= t).astype(np.float32)
    cur = (s <= t).astype(np.float32)
    mask_att = np.concatenate([prev, cur, prev, cur], axis=1).astype(np.float32)
    mask_scan = np.ones((128, 512), np.float32)
    mask_scan[:, ::64] = 0.0
    return ident, mask_hg, mask_att, mask_scan


_NC_CACHE = {}


def kernel(x, c, w_ada, b_ada, norm1_g, w_in, hg_lb_logits, hg_onorm_g, att_onorm_g,
           w_out, norm2_g, w_gate_up, w_down, final_g):
    f32 = lambda a: np.ascontiguousarray(np.asarray(a, dtype=np.float32))
    x = f32(x); c = f32(c)
    B = x.shape[0]
    ident, mask_hg, mask_att, mask_scan = _consts()
    if "nc" not in _NC_CACHE:
        _NC_CACHE["nc"] = build_nc()
    nc = _NC_CACHE["nc"]
    shared = {
        "b_ada": f32(b_ada).reshape(1, 6 * D),
        "w_ada": f32(w_ada).reshape(D, 6 * D),
        "w_in": f32(w_in).reshape(D, 3584),
        "w_out": f32(w_out).reshape(D, D),
        "w_gu": f32(w_gate_up).reshape(D, 2 * DFF),
        "w_down": f32(w_down).reshape(DFF, D),
        "final_g": f32(final_g).reshape(1, D),
        "ident": ident, "mask_hg": mask_hg, "mask_att": mask_att, "mask_scan": mask_scan,
    }
    in_maps = []
    for b in range(B):
        vecs = np.concatenate([
            c[b].reshape(8, 128),
            f32(b_ada).reshape(48, 128),
            f32(norm1_g).reshape(8, 128),
            f32(norm2_g).reshape(8, 128),
            f32(hg_lb_logits).reshape(8, 128),
            f32(att_onorm_g).reshape(4, 128),
            f32(hg_onorm_g).reshape(1, 128),
        ], axis=0)
        m = dict(shared)
        m["x"] = x[b]
        m["vecs"] = np.ascontiguousarray(vecs)
        in_maps.append(m)
    res = run_bass_kernel_spmd(nc, in_maps, core_ids=list(range(B)))
    out = np.stack([np.asarray(r["out"], dtype=np.float32) for r in res.results], axis=0)
    return out
```
